# Optimizing a Trainium2 kernel written in Bass

```python
import jax, jax.numpy as jnp
from jax import lax
import numpy as np

D_MODEL = 2048
BATCH = 2
SEQ = 16384
DEPTH = 1
DEC_BATCH = 8
DEC_SEQ = 64
PAST_LEN = 1024

CHUNK = 64
HEAD_DIM = 64
D_MIX = D_MODEL
D_A = D_MIX // 2
D_B = D_MIX - D_A
N_HEADS_A = D_A // HEAD_DIM
N_HEADS_B = D_B // HEAD_DIM
N_KV_B = N_HEADS_B // 4
GROUP_B = N_HEADS_B // N_KV_B
KV_B = N_KV_B * HEAD_DIM
PAST_CHUNKS_A = 8
WINDOW_B = 128
PAST_CHUNKS_B = WINDOW_B // CHUNK
PAST_ROWS_A = PAST_CHUNKS_A * CHUNK
PAST_ROWS_B = PAST_CHUNKS_B * CHUNK
REL_CLIP = 256
N_REL = REL_CLIP + CHUNK
SPLIT_SIZES = (D_A, D_A, D_A, D_A, D_B, KV_B, KV_B, D_B)
D_IN = sum(SPLIT_SIZES)
RMS_EPS = 1e-6
NEG_INF = -1e30

kernel_name = 'chunk_relpos_swa_sink_hybrid_step'


def rms_norm(x, g):
    xf = x.astype(jnp.float32)
    y = xf * lax.rsqrt(jnp.mean(xf * xf, axis=-1, keepdims=True) + RMS_EPS)
    return (y * g.astype(jnp.float32)).astype(x.dtype)


def band_distances(past_rows):
    return past_rows + np.arange(CHUNK)[:, None] - np.arange(past_rows + CHUNK)[None, :]


def rel_position_bias(table):
    idx = np.clip(band_distances(PAST_ROWS_A), -(CHUNK - 1), REL_CLIP) + (CHUNK - 1)
    return table.astype(jnp.float32)[:, idx][:, None]


def alibi_bias():
    d = np.abs(band_distances(PAST_ROWS_B)).astype(np.float32)
    slopes = (2.0 ** (-8.0 * np.arange(1, N_HEADS_B + 1, dtype=np.float32) / N_HEADS_B)).astype(np.float32)
    bias = -slopes[:, None, None] * d[None]
    return jnp.asarray(bias.reshape(N_KV_B, GROUP_B, CHUNK, -1), dtype=jnp.float32)


def key_validity(past_rows, n_hist, n_new, n_rows):
    idx = np.arange(n_rows)
    return jnp.asarray(((idx >= past_rows - n_hist) & (idx < past_rows + n_new))[None, :])


def chunk_band_attention(q, k_full, v_full, key_valid, bias, sinks):
    band = bias.shape[-1]
    n_chunks = q.shape[1] // CHUNK
    scale = HEAD_DIM ** -0.5

    def one_chunk(c):
        start = c * CHUNK
        qc = lax.dynamic_slice_in_dim(q, start, CHUNK, axis=1)
        kc = lax.dynamic_slice_in_dim(k_full, start, band, axis=1)
        vc = lax.dynamic_slice_in_dim(v_full, start, band, axis=1)
        ok = lax.dynamic_slice_in_dim(key_valid, start, band, axis=1)
        s = jnp.einsum('bqhgd,bkhd->bhgqk', qc, kc).astype(jnp.float32) * scale + bias
        s = jnp.where(ok[:, None, None, None, :], s, NEG_INF)
        if sinks is None:
            p = jax.nn.softmax(s, axis=-1)
        else:
            sink = sinks.astype(jnp.float32)[None, :, :, None, None]
            m = jnp.maximum(jnp.max(s, axis=-1, keepdims=True), sink)
            e = jnp.exp(s - m)
            p = e / (jnp.sum(e, axis=-1, keepdims=True) + jnp.exp(sink - m))
        return jnp.einsum('bhgqk,bkhd->bqhgd', p.astype(vc.dtype), vc)

    out = lax.map(one_chunk, jnp.arange(n_chunks))
    return jnp.moveaxis(out, 0, 1).reshape(q.shape)


def mixer_layer(x, hist_ak, hist_av, hist_bk, hist_bv, valid_a, valid_b,
                norm_g, w_in, rel_table, sinks, w_out):
    b, s, _ = x.shape
    h = rms_norm(x, norm_g)
    z = jnp.einsum('bsd,de->bse', h, w_in)
    offsets = list(np.cumsum(SPLIT_SIZES)[:-1])
    qa, ka, va, ga, qb, kb, vb, gb = jnp.split(z, offsets, axis=-1)
    qa = qa.reshape(b, s, N_HEADS_A, 1, HEAD_DIM)
    ka = ka.reshape(b, s, N_HEADS_A, HEAD_DIM)
    va = va.reshape(b, s, N_HEADS_A, HEAD_DIM)
    qb = qb.reshape(b, s, N_KV_B, GROUP_B, HEAD_DIM)
    kb = kb.reshape(b, s, N_KV_B, HEAD_DIM)
    vb = vb.reshape(b, s, N_KV_B, HEAD_DIM)
    oa = chunk_band_attention(qa, jnp.concatenate([hist_ak, ka], axis=1),
                              jnp.concatenate([hist_av, va], axis=1),
                              valid_a, rel_position_bias(rel_table), None)
    ob = chunk_band_attention(qb, jnp.concatenate([hist_bk, kb], axis=1),
                              jnp.concatenate([hist_bv, vb], axis=1),
                              valid_b, alibi_bias(), sinks.reshape(N_KV_B, GROUP_B))
    o = jnp.concatenate([oa.reshape(b, s, D_A) * jax.nn.silu(ga),
                         ob.reshape(b, s, D_B) * jax.nn.silu(gb)], axis=-1)
    y = x + jnp.einsum('bse,ed->bsd', o, w_out)
    return y, ka, va, kb, vb


def setup_inputs(seed: int = 0) -> dict:
    key = jax.random.key(seed)
    ks = jax.random.split(key, 12)
    la = min(PAST_ROWS_A, PAST_LEN)
    lb = min(PAST_ROWS_B, PAST_LEN)
    f32 = jnp.float32
    return {
        'x_prompt': jax.random.normal(ks[0], (BATCH, SEQ, D_MODEL), f32),
        'x_sample': jax.random.normal(ks[1], (DEC_BATCH, DEC_SEQ, D_MODEL), f32),
        'cache_a_k': jax.random.normal(ks[2], (DEPTH, DEC_BATCH, la, N_HEADS_A, HEAD_DIM), f32),
        'cache_a_v': jax.random.normal(ks[3], (DEPTH, DEC_BATCH, la, N_HEADS_A, HEAD_DIM), f32),
        'cache_b_k': jax.random.normal(ks[4], (DEPTH, DEC_BATCH, lb, N_KV_B, HEAD_DIM), f32),
        'cache_b_v': jax.random.normal(ks[5], (DEPTH, DEC_BATCH, lb, N_KV_B, HEAD_DIM), f32),
        'norm_in': 1.0 + 0.05 * jax.random.normal(ks[6], (DEPTH, D_MODEL), f32),
        'w_in': jax.random.normal(ks[7], (DEPTH, D_MODEL, D_IN), f32) * D_MODEL ** -0.5,
        'rel_bias_a': 0.2 * jax.random.normal(ks[8], (DEPTH, N_HEADS_A, N_REL), f32),
        'sinks_b': jax.random.normal(ks[9], (DEPTH, N_HEADS_B), f32),
        'w_out': jax.random.normal(ks[10], (DEPTH, D_MIX, D_MODEL), f32) * D_MIX ** -0.5,
        'norm_final': 1.0 + 0.05 * jax.random.normal(ks[11], (D_MODEL,), f32),
    }


def reference(x_prompt, x_sample, cache_a_k, cache_a_v, cache_b_k, cache_b_v,
              norm_in, w_in, rel_bias_a, sinks_b, w_out, norm_final):
    b_p, seq, _ = x_prompt.shape
    b_s, n_new, _ = x_sample.shape
    la = cache_a_k.shape[2]
    lb = cache_b_k.shape[2]
    s_pad = -(-n_new // CHUNK) * CHUNK
    keep_a = min(PAST_ROWS_A, seq)
    keep_b = min(PAST_ROWS_B, seq)

    valid_a_p = key_validity(PAST_ROWS_A, 0, seq, PAST_ROWS_A + seq)
    valid_b_p = key_validity(PAST_ROWS_B, 0, seq, PAST_ROWS_B + seq)
    valid_a_s = key_validity(PAST_ROWS_A, la, n_new, PAST_ROWS_A + s_pad)
    valid_b_s = key_validity(PAST_ROWS_B, lb, n_new, PAST_ROWS_B + s_pad)

    hp = x_prompt
    hs = jnp.pad(x_sample, ((0, 0), (0, s_pad - n_new), (0, 0)))
    ak_p, av_p, bk_p, bv_p = [], [], [], []
    ak_s, av_s, bk_s, bv_s = [], [], [], []
    for l in range(DEPTH):
        za = jnp.zeros((b_p, PAST_ROWS_A, N_HEADS_A, HEAD_DIM), hp.dtype)
        zb = jnp.zeros((b_p, PAST_ROWS_B, N_KV_B, HEAD_DIM), hp.dtype)
        hp, ka, va, kb, vb = mixer_layer(hp, za, za, zb, zb, valid_a_p, valid_b_p,
                                         norm_in[l], w_in[l], rel_bias_a[l], sinks_b[l], w_out[l])
        ak_p.append(ka[:, seq - keep_a:])
        av_p.append(va[:, seq - keep_a:])
        bk_p.append(kb[:, seq - keep_b:])
        bv_p.append(vb[:, seq - keep_b:])
        pad_a = ((0, 0), (PAST_ROWS_A - la, 0), (0, 0), (0, 0))
        pad_b = ((0, 0), (PAST_ROWS_B - lb, 0), (0, 0), (0, 0))
        hs, ka, va, kb, vb = mixer_layer(hs, jnp.pad(cache_a_k[l], pad_a), jnp.pad(cache_a_v[l], pad_a),
                                         jnp.pad(cache_b_k[l], pad_b), jnp.pad(cache_b_v[l], pad_b),
                                         valid_a_s, valid_b_s,
                                         norm_in[l], w_in[l], rel_bias_a[l], sinks_b[l], w_out[l])
        ak_s.append(jnp.concatenate([cache_a_k[l], ka[:, :n_new]], axis=1)[:, n_new:])
        av_s.append(jnp.concatenate([cache_a_v[l], va[:, :n_new]], axis=1)[:, n_new:])
        bk_s.append(jnp.concatenate([cache_b_k[l], kb[:, :n_new]], axis=1)[:, n_new:])
        bv_s.append(jnp.concatenate([cache_b_v[l], vb[:, :n_new]], axis=1)[:, n_new:])

    y_prompt = rms_norm(hp, norm_final)
    y_sample = rms_norm(hs[:, :n_new], norm_final)
    return (y_prompt, y_sample,
            jnp.stack(ak_p), jnp.stack(av_p), jnp.stack(bk_p), jnp.stack(bv_p),
            jnp.stack(ak_s), jnp.stack(av_s), jnp.stack(bk_s), jnp.stack(bv_s))
```

```python
from contextlib import ExitStack
import numpy as np
import concourse.bass as bass
import concourse.mybir as mybir
from concourse.bass_utils import run_bass_kernel_spmd

F32 = mybir.dt.float32
BF16 = mybir.dt.bfloat16
AF = mybir.ActivationFunctionType
ALU = mybir.AluOpType
AX = mybir.AxisListType

ENG_ATTR = {"pe": "tensor", "act": "scalar", "dve": "vector", "pool": "gpsimd", "sp": "sync"}


class _Op:
    __slots__ = ("eng", "fn", "dma", "idx", "deps", "dmawaits", "milestone", "cnt")


class _Rec:
    def __getattr__(self, name):
        def call(*a, **k):
            self.__dict__["rec"] = (name, a, k)
            return None
        return call


class Sched:
    def __init__(self):
        self.ops = []
        self.last_write = {}
        self.readers = {}
        self.dma_cnt = {}
        self.dma_waiters = {}
        self.frozen = False
        import os as _os
        self.maxops = int(_os.environ.get("KMAXOPS", "100000000"))

    def add(self, eng, fn, reads=(), writes=(), dma=None):
        if self.frozen:
            return -1
        i = len(self.ops)
        if i >= self.maxops:
            self.frozen = True
            return -1
        op = _Op()
        rec = _Rec()
        fn(rec)
        op.eng, op.fn, op.dma, op.idx = eng, rec.__dict__["rec"], dma, i
        op.milestone = False
        op.cnt = 0
        deps = {}
        for r in reads:
            j = self.last_write.get(r)
            if j is not None:
                deps[j] = "raw"
        for w in writes:
            j = self.last_write.get(w)
            if j is not None and j not in deps:
                deps[j] = "waw"
            for k in self.readers.get(w, ()):
                if k not in deps:
                    deps[k] = "war"
        if dma is not None:
            for k in self.dma_waiters.get(dma, ()):
                if k not in deps:
                    deps[k] = "war"
            self.dma_waiters[dma] = []
        op.deps = []
        op.dmawaits = []
        latest = {}
        for j, kind in deps.items():
            oj = self.ops[j]
            if oj.dma is not None:
                op.dmawaits.append((oj.dma, self.dma_cnt[oj.dma]))
                self.dma_waiters.setdefault(oj.dma, []).append(i)
            else:
                if oj.eng == eng and dma is None and eng == "pe":
                    continue
                if latest.get(oj.eng, -1) < j:
                    latest[oj.eng] = j
        for j in latest.values():
            self.ops[j].milestone = True
            op.deps.append(j)
        if dma is not None:
            self.dma_cnt[dma] = self.dma_cnt.get(dma, 0) + 16
        for r in reads:
            self.readers.setdefault(r, []).append(i)
        for w in writes:
            self.last_write[w] = i
            self.readers[w] = []
        self.ops.append(op)
        return i

    def emit(self, nc):
        cnt = {e: 0 for e in ENG_ATTR}
        for op in self.ops:
            if op.milestone:
                cnt[op.eng] += 1
                op.cnt = cnt[op.eng]
        with ExitStack() as es:
            sems = {}
            for e in ENG_ATTR:
                if cnt[e] > 0:
                    sems[("eng", e)] = es.enter_context(nc.semaphore("s_" + e))
            for n, k in enumerate(self.dma_cnt):
                sems[("dma", k)] = es.enter_context(nc.semaphore("d_%d" % n))
            block = es.enter_context(nc.Block())
            for eng, attr in ENG_ATTR.items():
                ops_e = [op for op in self.ops if op.eng == eng]

                def body(e, ops_e=ops_e, eng=eng):
                    waited = {}
                    for op in ops_e:
                        need = {}
                        for j in op.deps:
                            oj = self.ops[j]
                            key = ("eng", oj.eng)
                            need[key] = max(need.get(key, 0), oj.cnt)
                        for (k, v) in op.dmawaits:
                            key = ("dma", k)
                            need[key] = max(need.get(key, 0), v)
                        for key, v in need.items():
                            if waited.get(key, 0) < v:
                                e.wait_ge(sems[key], v)
                                waited[key] = v
                        name, a, k = op.fn
                        ins = getattr(e, name)(*a, **k)
                        if op.dma is not None:
                            ins.then_inc(sems[("dma", op.dma)], 16)
                        elif op.milestone:
                            ins.then_inc(sems[("eng", eng)], 1)
                    if eng == "sp":
                        for k, v in self.dma_cnt.items():
                            e.wait_ge(sems[("dma", k)], v)

                getattr(block, attr)(body)
        return cnt


D = 2048
D_IN = 6656
NBLK_OWN = 32
SAMPLE_BLK = 36
XROWS = 37 * 128
YROWS = 4096 + 64
OFF_QA, OFF_KA, OFF_VA, OFF_GA, OFF_QB, OFF_KB, OFF_VB, OFF_GB = 0, 1024, 2048, 3072, 4096, 5120, 5376, 5632
NEG = -30000.0
BATCH_B = False
FM_SPLIT = 8
EPS = 1e-6

ST_BLOCKS = [[0, 1, 2, 3, 4, 5, 6, SAMPLE_BLK], list(range(7, 15)), list(range(15, 23)),
             list(range(23, 31)), list(range(31, 36))]


def build_program(interleave=True, stop=None):
    import os
    if stop is None:
        stop = int(os.environ.get("KSTOP", "999"))
    nc = bass.Bass("TRN2", target_bir_lowering=False)
    din = lambda n, s: nc.dram_tensor(n, s, F32, kind="ExternalInput").ap()
    dout = lambda n, s: nc.dram_tensor(n, s, F32, kind="ExternalOutput").ap()
    xin = din("xin", [XROWS, D])
    w_in = din("w_in", [D, D_IN])
    w_out = din("w_out", [D, D])
    g_inT_d = din("g_inT", [128, 16])
    gfin_d = din("gfin", [128, D])
    relb_d = din("relb", [16, 320])
    sinks_d = din("sinks", [128, 16])
    cak, cav = din("cak", [512, 1024]), din("cav", [512, 1024])
    cbk, cbv = din("cbk", [128, 256]), din("cbv", [128, 256])
    hvalid_d = din("hvalid", [128, 4])
    alibi_d = din("alibi", [128, 16, 2, 128])
    maskA_d = din("maskA", [128, 5, 128])
    ident_d = din("ident", [128, 128])
    perm_d = din("perm", [128, 128])
    antiI_d = din("antiI", [128, 128])
    y_d = dout("y", [YROWS, D])
    kvp_a = dout("kvp_a", [2, 512, 1024])
    kvp_b = dout("kvp_b", [2, 128, 256])
    kvs_a = dout("kvs_a", [2, 512, 1024])
    kvs_b = dout("kvs_b", [2, 128, 256])
    ext_h = nc.dram_tensor("ext_tab", [16, 768], F32, kind="Internal")
    ext_d = ext_h.ap()
    biasA_d = nc.dram_tensor("biasA_s", [16, 128, 640], F32, kind="Internal").ap()

    w_in_v = w_in.rearrange("(kc p) c -> p kc c", p=128)
    w_out_v = w_out.rearrange("(kc p) c -> p kc c", p=128)

    S = Sched()

    def ckpt(level):
        if os.environ.get("KPRINT"):
            print("ckpt", level, "nops", len(S.ops))
        if level >= stop:
            S.frozen = True

    es = ExitStack()
    with es:
        sb = lambda n, s, d: es.enter_context(nc.sbuf_tensor("s_" + n, s, d))
        pt = lambda n, s, d: es.enter_context(nc.psum_tensor("p_" + n, s, d))
        hT = sb("hT", [128, 16, 1024], BF16)
        oT = sb("oT", [128, 16, 1024], BF16)
        wsl = [sb("wsl%d" % i, [128, 16, 512], BF16) for i in range(2)]
        kTc = sb("kTc", [128, 12, 512], BF16)
        vca = sb("vca", [128, 4, 16, 66], BF16)
        vcb = sb("vcb", [128, 4, 66], BF16)
        qT_w = [sb("qTw%d" % i, [128, 1024], BF16) for i in range(2)]
        kT_w = [sb("kTw%d" % i, [128, 1024], BF16) for i in range(2)]
        gT_w = [sb("gTw%d" % i, [128, 1024], BF16) for i in range(2)]
        v_w = [sb("vw%d" % i, [128, 8, 2, 66], BF16) for i in range(2)]
        kTb_w = sb("kTbw", [128, 4, 1024], BF16)
        vT_w = sb("vTw", [128, 1024], BF16)
        ident_bf = sb("ident_bf", [128, 128], BF16)
        vb_w = sb("vbw", [128, 8, 4, 66], BF16)
        t_w = [sb("tw%d" % i, [128, 640], F32) for i in range(2)]
        p_w = [sb("pw%d" % i, [128, 640], BF16) for i in range(3)]
        o_tm = [sb("otm%d" % i, [128, 128], F32) for i in range(2)]
        rinv = [sb("rinv%d" % i, [128, 2], F32) for i in range(4)]
        bA_w = [sb("bAw%d" % i, [128, 2, 640], F32) for i in range(2)]
        bB_w = [sb("bBw%d" % i, [128, 2, 2, 128], F32) for i in range(2)]
        gfin = sb("gfin", [128, D], F32)
        xs = sb("xs", [128, D], F32)
        xs_main = xs
        junk = sb("junk", [128, 512], BF16)
        xr = [sb("xr%d" % i, [128, 512], F32) for i in range(3)]
        rD = sb("rD", [128, D], F32)
        kvst = [sb("kvst%d" % i, [128, 512], F32) for i in range(2)]
        ident = sb("ident", [128, 128], F32)
        perm_f = kvst[1][:, 0:128]
        antiI = kvst[1][:, 128:256]
        perm = sb("perm", [128, 128], BF16)
        g_inT = sb("g_inT", [128, 16], F32)
        esink = sb("esink", [128, 16], F32)
        hvalid = sb("hvalid", [128, 4], F32)
        stat = sb("stat", [128, 64], F32)
        ssC = sb("ssC", [128, 8, 4], F32)
        ext_sb = rD[0:16, 0:768]

        pz = pt("pz", [128, 2, 512], F32)
        pv = pt("pv", [128, 512], F32)
        pst = pt("pst", [128, 1, 1536], F32)
        ppv = pt("ppv", [128, 2, 512], F32)

        maskA = kTb_w[:].rearrange("p a b -> p (a b)").bitcast(F32)[:, 0:640]

        def dma(q, out, in_, reads=(), writes=(), key=None):
            S.add(q, lambda e: e.dma_start(out=out, in_=in_), reads=reads, writes=writes, dma=key)

        dma("sp", ident[:], ident_d, writes=["ident"], key="pro")
        dma("sp", perm_f, perm_d, writes=["perm_f"], key="pro")
        dma("sp", antiI, antiI_d, writes=["antiI"], key="pro")
        dma("sp", g_inT[:], g_inT_d, writes=["g_inT"], key="pro")
        dma("sp", esink[:], sinks_d, writes=["esink"], key="pro")
        dma("sp", hvalid[:], hvalid_d, writes=["hvalid"], key="pro")
        dma("sp", maskA, maskA_d.rearrange("p a b -> p (a b)"), writes=["maskA", "kTbw"], key="pro")
        dma("sp", rD[0:16, 64:384], relb_d, writes=["ext_sb", "rD"], key="pro")
        dma("sp", gfin[:], gfin_d, writes=["gfin"], key="pro")
        S.add("dve", lambda e: e.tensor_copy(out=perm[:], in_=perm_f), reads=["perm_f"], writes=["perm"])
        S.add("dve", lambda e: e.tensor_copy(out=ident_bf[:], in_=ident[:]), reads=["ident"], writes=["ident_bf"])
        S.add("act", lambda e: e.activation(out=esink[:], in_=esink[:], func=AF.Exp),
              reads=["esink"], writes=["esink"])
        S.add("dve", lambda e: e.tensor_scalar(out=esink[:], in0=esink[:], scalar1=2.0, scalar2=None, op0=ALU.mult),
              reads=["esink"], writes=["esink"])
        S.add("dve", lambda e: e.tensor_copy(out=rD[0:16, 0:64], in_=rD[0:16, 64:128]),
              reads=["ext_sb"], writes=["ext_sb0"])
        S.add("dve", lambda e: e.tensor_copy(out=rD[0:16, 384:768],
                                             in_=rD[0:16, 383:384].to_broadcast([16, 384])),
              reads=["ext_sb"], writes=["ext_sb1"])
        dma("sp", ext_d, rD[0:16, 0:768], reads=["ext_sb", "ext_sb0", "ext_sb1", "rD"], writes=["ext_d"], key="pro2")
        def bias_round(r):
            Wv = oT[:].rearrange("p a b -> p (a b)").bitcast(F32)[:, 0:5120].rearrange("p (h u) -> p h u", u=640)
            src = bass.AP(tensor=ext_h, offset=r * 8 * 768, ap=[[1, 128], [768, 8], [1, 640]])
            dma("sp", Wv, src, reads=["ext_d"], writes=["Wall"] + [("oT", lb) for lb in range(8)], key="pb")
            for hh in range(8):
                h = 8 * r + hh
                b = h % 2
                for jj in range(5):
                    S.add("pe", lambda e, jj=jj: e.matmul(
                        out=pst[:, 0, jj * 128:(jj + 1) * 128], lhsT=antiI,
                        rhs=Wv[:, hh, 512 - 128 * jj:640 - 128 * jj], start=True, stop=True),
                        reads=["Wall", "antiI"] + [("oT", lb) for lb in range(8)], writes=[("pst", jj // 4)])
                S.add("dve", lambda e: e.tensor_tensor(out=t_w[b][:], in0=pst[:, 0, 0:640], in1=maskA, op=ALU.add),
                      reads=[("pst", 0), ("pst", 1), "maskA", "kTbw"], writes=[("tw", b)])
                dma("sp", biasA_d[h], t_w[b][:], reads=[("tw", b)], writes=[("biasA_d", h)], key=("pb2", b))

        ckpt(1)
        S.add("dve", lambda e: e.memset(vca[:, :, :, 64:66], 2.0), writes=["vca1"])
        S.add("dve", lambda e: e.memset(vcb[:, :, 64:66], 2.0), writes=["vcb1"])
        S.add("dve", lambda e: e.memset(vb_w[:, :, :, 64:66], 2.0), writes=["vbw1"])
        for i in range(2):
            S.add("dve", lambda e, i=i: e.memset(v_w[i][:, :, :, 64:66], 2.0), writes=[("vw1", i)])
        for lb in range(4):
            for i in range(2):
                S.add("dve", lambda e, i=i, lb=lb: e.tensor_copy(
                    out=v_w[i][:, lb, :, 64:66], in_=hvalid[:, lb:lb + 1].unsqueeze(1).to_broadcast([128, 2, 2])),
                    reads=["hvalid", ("vw1", i)], writes=[("vw1", i)])
            S.add("dve", lambda e, lb=lb: e.tensor_copy(
                out=vb_w[:, lb, :, 64:66], in_=hvalid[:, lb:lb + 1].unsqueeze(1).to_broadcast([128, 4, 2])),
                reads=["hvalid", "vbw1"], writes=["vbw1"])

        def cache_to_carry():
            xs4 = xs[:].rearrange("p (a c) -> p a c", a=4)
            for half in range(2):
                dma("sp", xs4, cak.rearrange("(a p) c -> p a c", p=128)[:, :, half * 512:(half + 1) * 512],
                    writes=["xs"], key="xs")
                for pp in range(4):
                    pair = half * 4 + pp
                    for blk in range(4):
                        S.add("pe", lambda e, pp=pp, blk=blk: e.transpose(
                            out=pst[:, 0, blk * 128:(blk + 1) * 128], in_=xs4[:, blk, pp * 128:(pp + 1) * 128],
                            identity=ident[:]), reads=["xs", "ident"], writes=[("pst", 0)])
                    S.add("dve", lambda e, pair=pair: e.tensor_copy(out=kTc[:, pair, :], in_=pst[:, 0, 0:512]),
                          reads=[("pst", 0)], writes=[("kTc", pair)])
                dma("sp", xs4, cav.rearrange("(a p) c -> p a c", p=128)[:, :, half * 512:(half + 1) * 512],
                    writes=["xs"], key="xs")
                for blk in range(4):
                    S.add("dve", lambda e, blk=blk, half=half: e.tensor_copy(
                        out=vca[:, blk, half * 8:(half + 1) * 8, 0:64],
                        in_=xs4[:, blk, :].rearrange("p (h d) -> p h d", d=64)),
                        reads=["xs", "vca1"], writes=[("vca_c", half * 4 + pp) for pp in range(4)])
            dma("sp", xs[:, 0:256], cbk, writes=["xs"], key="xs")
            dma("sp", xs[:, 256:512], cbv, writes=["xsb"], key="xs")
            for ch in range(2):
                S.add("pe", lambda e, ch=ch: e.transpose(out=pst[:, 0, ch * 128:(ch + 1) * 128],
                                                         in_=xs[:, ch * 128:(ch + 1) * 128], identity=ident[:]),
                      reads=["xs", "ident"], writes=[("pst", 0)])
            S.add("dve", lambda e: e.tensor_copy(out=kTc[:, 8:10, 384:512],
                                                 in_=pst[:, 0, 0:256].rearrange("p (c k) -> p c k", c=2)),
                  reads=[("pst", 0)], writes=[("kTc", 8)])
            for ch in range(2):
                S.add("pe", lambda e, ch=ch: e.matmul(out=pst[:, 0, 512 + ch * 128:512 + (ch + 1) * 128], lhsT=perm[:],
                                                      rhs=kTc[:, 8 + ch, 384:512], start=True, stop=True),
                      reads=[("kTc", 8), "perm"], writes=[("pst", 1)])
            S.add("dve", lambda e: e.tensor_copy(out=kTc[:, 10:12, 384:512],
                                                 in_=pst[:, 0, 512:768].rearrange("p (c k) -> p c k", c=2)),
                  reads=[("pst", 1)], writes=[("kTc", 8)])
            S.add("dve", lambda e: e.tensor_copy(out=vcb[:, :, 0:64],
                                                 in_=xs[:, 256:512].rearrange("p (h d) -> p h d", d=64)),
                  reads=["xsb", "vcb1"], writes=["vcb_c"])
            dma("sp", kvs_a[0, 0:448, :], cak[64:512, :], key="kvroll")
            dma("sp", kvs_a[1, 0:448, :], cav[64:512, :], key="kvroll")
            dma("sp", kvs_b[0, 0:64, :], cbk[64:128, :], key="kvroll")
            dma("sp", kvs_b[1, 0:64, :], cbv[64:128, :], key="kvroll")


        ckpt(2)
        stat_col = [0]

        def new_stat(n=1):
            c = stat_col[0]
            if c + n > 48:
                c = 0
            stat_col[0] = c + n
            return c

        def rstd_from(ss_ap_fn, ss_res, out_res):
            c = new_stat()
            S.add("dve", lambda e: e.tensor_scalar(out=stat[:, c:c + 1], in0=ss_ap_fn(), scalar1=1.0 / D,
                                                   scalar2=EPS, op0=ALU.mult, op1=ALU.add),
                  reads=[ss_res], writes=[("stat", c)])
            S.add("act", lambda e: e.activation(out=stat[:, c:c + 1], in_=stat[:, c:c + 1], func=AF.Sqrt),
                  reads=[("stat", c)], writes=[("stat", c)])
            S.add("dve", lambda e: e.reciprocal(out=stat[:, c:c + 1], in_=stat[:, c:c + 1]),
                  reads=[("stat", c)], writes=[("stat", c), out_res])
            return c

        def phase_A_parts(gb, lb, alt=False):
            xs, xsr, xkey = (rD, "rD", "xs2") if alt else (xs_main, "xs", "xs")

            def prep_dma():
                dma("sp", xs[:], xin[gb * 128:(gb + 1) * 128, :], writes=[xsr], key=xkey)

            def prep():
                c0 = new_stat()
                cfirst = new_stat(4)
                for q4 in range(4):
                    cq = cfirst + q4
                    S.add("act", lambda e, q4=q4, cq=cq: e.activation(
                        out=junk[:], in_=xs[:, q4 * 512:(q4 + 1) * 512], func=AF.Square,
                        accum_out=stat[:, cq:cq + 1]), reads=[xsr], writes=["junk", ("stat", cq)])
                S.add("dve", lambda e: e.tensor_reduce(out=stat[:, c0:c0 + 1], in_=stat[:, cfirst:cfirst + 4],
                                                       axis=AX.X, op=ALU.add),
                      reads=[("stat", cfirst + i) for i in range(4)], writes=[("stat", c0)])
                cr = rstd_from(lambda: stat[:, c0:c0 + 1], ("stat", c0), ("rstdA", lb))
                S.add("act", lambda e: e.activation(out=xs[:], in_=xs[:], func=AF.Copy, scale=stat[:, cr:cr + 1]),
                      reads=[xsr, ("stat", cr)], writes=[xsr])

            def group(grp):
                buf = grp % 2
                for k4 in range(4):
                    kc = grp * 4 + k4
                    S.add("pe", lambda e, kc=kc, k4=k4: e.transpose(
                        out=pst[:, 0, buf * 512 + k4 * 128:buf * 512 + (k4 + 1) * 128], in_=xs[:, kc * 128:(kc + 1) * 128],
                        identity=ident[:]), reads=[xsr, "ident"], writes=[("pst", buf)])
                S.add("dve", lambda e: e.tensor_tensor(
                    out=hT[:, grp * 4:(grp + 1) * 4, lb * 128:(lb + 1) * 128],
                    in0=pst[:, 0, buf * 512:(buf + 1) * 512].rearrange("p (c k) -> p c k", c=4),
                    in1=g_inT[:, grp * 4:(grp + 1) * 4].unsqueeze(2).to_broadcast([128, 4, 128]),
                    op=ALU.mult), reads=[("pst", buf), "g_inT"], writes=[("hT", lb)])

            return prep_dma, prep, [(lambda grp=grp: group(grp)) for grp in range(4)]

        def phase_A_block(gb, lb, alt=False):
            prep_dma, prep, groups = phase_A_parts(gb, lb, alt)
            prep_dma()
            prep()
            for g in groups:
                g()


        wslot_ctr = [0]

        def load_weights(parts):
            s = wslot_ctr[0] % 2
            wslot_ctr[0] += 1
            for (d0, n, view, c0) in parts:
                S.add("pool", lambda e, d0=d0, n=n, view=view, c0=c0, s=s: e.dma_start(
                    out=wsl[s][:, :, d0:d0 + n], in_=view[:, :, c0:c0 + n]),
                    writes=[("wsl", s)], dma=("w", s))
            return s

        pz_ctr = [0]

        def groups_of(c0, c1, g=512):
            out = []
            c = c0
            while c < c1:
                n = min(g, c1 - c)
                out.append((c, n))
                c += n
            return out

        def fm_proj(slot, wc0, c0, c1, evac, hreads):
            items = []
            for (n0, nn) in groups_of(c0, c1):
                gst = {}
                for part in range(FM_SPLIT):
                    def item(n0=n0, nn=nn, part=part, gst=gst):
                        if part == 0:
                            gst["b"] = pz_ctr[0] % 2
                            pz_ctr[0] += 1
                        b = gst["b"]
                        per = 16 // FM_SPLIT
                        for kc in range(part * per, (part + 1) * per):
                            S.add("pe", lambda e, kc=kc, b=b: e.matmul(
                                out=pz[:, b, 0:nn], lhsT=wsl[slot][:, kc, wc0:wc0 + 128], rhs=hT[:, kc, n0:n0 + nn],
                                start=(kc == 0), stop=(kc == 15)),
                                reads=[("wsl", slot)] + hreads(n0, nn), writes=[("pz", b)])
                        if part == FM_SPLIT - 1:
                            evac(b, n0, nn)
                    items.append(item)
            return items

        def hreads_blocks(n0, nn):
            return [("hT", lb) for lb in range(n0 // 128, (n0 + nn + 127) // 128)]

        n_st = len(ST_BLOCKS)
        pm_ctr = [0]
        st_ctr = [0]
        step_ctr = [0]
        kvst_ctr = [0]
        xr_ctr = [0]
        pv_ctr = [0]
        pending_D = []
        bias_ctr = {"A": 0, "Bq": 0}

        bias_round(0)
        for lb, gb in enumerate(ST_BLOCKS[0]):
            phase_A_block(gb, lb, alt=(lb % 2 == 1))
        bias_round(1)
        cache_to_carry()
        ckpt(3)

        def attention_units(st, wb, kind, pair_idx, qblocks, sample_lb, bias_tile, oc):
            units = []
            nkb = 5 if kind == "A" else 2
            ncol = nkb * 128
            if kind != "A" and BATCH_B:
                g = pair_idx // 2
                gch, gh = g // 2, g % 2
                for lbq in qblocks:
                    is_s = (lbq == sample_lb)

                    def kv_b(hh, jj, lbq=lbq, is_s=is_s):
                        lk = lbq - 1 + jj
                        r0 = hh * 64
                        ch = gch if hh == gh else 2 + gch
                        use_carry = False
                        if is_s:
                            if jj == 0:
                                use_carry = True
                            else:
                                lk = lbq
                        elif lk < 0:
                            use_carry = True
                        if use_carry:
                            return (kTc[r0:r0 + 64, 8 + ch, 384:512], vcb[:, g, 0:65], [("kTc", 8)], ["vcb_c"])
                        return (kTb_w[r0:r0 + 64, ch, lk * 128:(lk + 1) * 128], vb_w[:, lk, g, 0:65],
                                ["kTbw"], [("vbw", lk)])

                    state = {}

                    def front(lbq=lbq, kv_b=kv_b, state=state):
                        u = st_ctr[0]
                        st_ctr[0] += 1
                        sbuf = u % 2
                        pbuf = u % 3
                        state["sbuf"], state["pbuf"] = sbuf, pbuf
                        pc0 = sbuf * 512
                        for hh in range(2):
                            for jj in range(2):
                                kap, vap, kres, vres = kv_b(hh, jj)
                                oc0 = pc0 + (hh * 2 + jj) * 128
                                S.add("pe", lambda e, kap=kap, oc0=oc0, hh=hh: e.matmul(
                                    out=pst[:, 0, oc0:oc0 + 128], lhsT=kap,
                                    rhs=qT_w[wb][hh * 64:(hh + 1) * 64, lbq * 128:(lbq + 1) * 128],
                                    start=True, stop=True), reads=kres + [("qTw", wb)], writes=[("pst", sbuf)])
                        bap = bias_tile[:].rearrange("p h a b -> p (h a b)")
                        S.add("dve", lambda e: e.tensor_tensor(out=t_w[sbuf][:, 0:512], in0=pst[:, 0, pc0:pc0 + 512],
                                                               in1=bap, op=ALU.add),
                              reads=[("pst", sbuf), ("bias", kind, id(bias_tile))], writes=[("tw", sbuf), ("tw5", sbuf)])
                        S.add("act", lambda e: e.activation(out=p_w[pbuf][:, 0:512], in_=t_w[sbuf][:, 0:512], func=AF.Exp),
                              reads=[("tw", sbuf), ("tw5", sbuf)], writes=[("pw", pbuf)])

                    def back(lbq=lbq, kv_b=kv_b, state=state):
                        pbuf = state["pbuf"]
                        slot = pm_ctr[0] % 2
                        pm_ctr[0] += 1
                        rv = slot
                        state["slot"] = slot
                        for hh in range(2):
                            for jj in range(2):
                                kap, vap, kres, vres = kv_b(hh, jj)
                                S.add("pe", lambda e, jj=jj, hh=hh, vap=vap: e.matmul(
                                    out=ppv[:, slot, hh * 66:hh * 66 + 65],
                                    lhsT=p_w[pbuf][:, (hh * 2 + jj) * 128:(hh * 2 + jj + 1) * 128],
                                    rhs=vap, start=(jj == 0), stop=(jj == 1)),
                                    reads=vres + [("pw", pbuf)], writes=[("ppv", slot)])
                        ob = lbq % 2
                        den = ppv[:, slot, 0:132].rearrange("p (h c) -> p h c", c=66)[:, :, 64]
                        S.add("dve", lambda e: e.tensor_tensor(out=rinv[rv][:, 0:2], in0=den,
                                                               in1=esink[:, 2 * pair_idx:2 * pair_idx + 2], op=ALU.add),
                              reads=[("ppv", slot), "esink"], writes=[("rinv", rv)])
                        S.add("dve", lambda e: e.reciprocal(out=rinv[rv][:, 0:2], in_=rinv[rv][:, 0:2]),
                              reads=[("rinv", rv)], writes=[("rinv", rv)])
                        for hh in range(2):
                            S.add("act", lambda e, hh=hh: e.activation(
                                out=o_tm[ob][:, hh * 64:(hh + 1) * 64], in_=ppv[:, slot, hh * 66:hh * 66 + 64],
                                func=AF.Copy, scale=rinv[rv][:, hh:hh + 1]),
                                reads=[("ppv", slot), ("rinv", rv)], writes=[("otm", ob, hh)])

                    def tail(lbq=lbq, state=state):
                        slot = state["slot"]
                        ob = lbq % 2
                        S.add("pe", lambda e: e.transpose(out=ppv[:, slot, 256:384], in_=o_tm[ob][:], identity=ident[:]),
                              reads=[("otm", ob, 0), ("otm", ob, 1), "ident"], writes=[("ppv", slot)])
                        S.add("dve", lambda e: e.tensor_tensor(
                            out=oT[:, oc, lbq * 128:(lbq + 1) * 128], in0=ppv[:, slot, 256:384],
                            in1=gT_w[wb][:, lbq * 128:(lbq + 1) * 128], op=ALU.mult),
                            reads=[("ppv", slot), ("gTw", wb)], writes=[("oT", lbq)])

                    units.append((front, back, tail))
                return units
            for lbq in qblocks:
                is_s = (lbq == sample_lb)
                for hh in range(2):
                    def kv_src(jj, lbq=lbq, hh=hh, is_s=is_s):
                        lk = lbq - (nkb - 1) + jj
                        r0 = hh * 64
                        use_carry = False
                        if is_s:
                            if jj < nkb - 1:
                                use_carry, cblk = True, (4 - (nkb - 1)) + jj
                            else:
                                lk = lbq
                        elif lk < 0:
                            use_carry, cblk = True, 4 + lk
                        if kind == "A":
                            if use_carry:
                                return (kTc[r0:r0 + 64, pair_idx, cblk * 128:(cblk + 1) * 128],
                                        vca[:, cblk, pair_idx * 2 + hh, 0:65], [("kTc", pair_idx)], [("vca_c", pair_idx)])
                            return (kT_w[wb][r0:r0 + 64, lk * 128:(lk + 1) * 128], v_w[wb][:, lk, hh, 0:65],
                                    [("kTw", wb)], [("vw", wb, lk)])
                        g = pair_idx // 2
                        gch, gh = g // 2, g % 2
                        ch = gch if hh == gh else 2 + gch
                        if use_carry:
                            return (kTc[r0:r0 + 64, 8 + ch, cblk * 128:(cblk + 1) * 128], vcb[:, g, 0:65],
                                    [("kTc", 8)], ["vcb_c"])
                        return (kTb_w[r0:r0 + 64, ch, lk * 128:(lk + 1) * 128], vb_w[:, lk, g, 0:65],
                                ["kTbw"], [("vbw", lk)])

                    state = {}

                    def front(lbq=lbq, hh=hh, kv_src=kv_src, state=state):
                        u = st_ctr[0]
                        st_ctr[0] += 1
                        sbuf = u % 2
                        pbuf = u % 3
                        s5 = u % 4
                        state["sbuf"] = sbuf
                        state["pbuf"] = pbuf
                        pc0 = sbuf * 512
                        for jj in range(nkb):
                            kap, vap, kres, vres = kv_src(jj)
                            if kind == "A" and jj == 4:
                                oc0, ores = 1024 + s5 * 128, ("pst", 2)
                            else:
                                oc0, ores = pc0 + jj * 128, ("pst", sbuf)
                            S.add("pe", lambda e, kap=kap, oc0=oc0: e.matmul(
                                out=pst[:, 0, oc0:oc0 + 128], lhsT=kap,
                                rhs=qT_w[wb][hh * 64:(hh + 1) * 64, lbq * 128:(lbq + 1) * 128],
                                start=True, stop=True), reads=kres + [("qTw", wb)], writes=[ores])
                        if kind == "A":
                            bap = bias_tile[:, hh, :]
                            S.add("dve", lambda e: e.tensor_tensor(out=t_w[sbuf][:, 0:512], in0=pst[:, 0, pc0:pc0 + 512],
                                                                   in1=bap[:, 0:512], op=ALU.add),
                                  reads=[("pst", sbuf), ("bias", kind, id(bias_tile))], writes=[("tw", sbuf)])
                            S.add("dve", lambda e: e.tensor_tensor(out=t_w[sbuf][:, 512:640],
                                                                   in0=pst[:, 0, 1024 + s5 * 128:1024 + (s5 + 1) * 128],
                                                                   in1=bap[:, 512:640], op=ALU.add),
                                  reads=[("pst", 2), ("bias", kind, id(bias_tile))], writes=[("tw5", sbuf)])
                        else:
                            bap = bias_tile[:, hh, :, :].rearrange("p a b -> p (a b)")
                            S.add("dve", lambda e: e.tensor_tensor(out=t_w[sbuf][:, 0:ncol], in0=pst[:, 0, pc0:pc0 + ncol],
                                                                   in1=bap, op=ALU.add),
                                  reads=[("pst", sbuf), ("bias", kind, id(bias_tile))], writes=[("tw", sbuf), ("tw5", sbuf)])
                        S.add("act", lambda e: e.activation(out=p_w[pbuf][:, 0:ncol], in_=t_w[sbuf][:, 0:ncol],
                                                            func=AF.Exp),
                              reads=[("tw", sbuf), ("tw5", sbuf)], writes=[("pw", pbuf)])

                    def back(lbq=lbq, hh=hh, kv_src=kv_src, state=state):
                        sbuf = state["sbuf"]
                        pbuf = state["pbuf"]
                        slot = pm_ctr[0] % 2
                        pm_ctr[0] += 1
                        rv = slot
                        for jj in range(nkb):
                            kap, vap, kres, vres = kv_src(jj)
                            S.add("pe", lambda e, jj=jj, vap=vap: e.matmul(
                                out=ppv[:, slot, 0:65], lhsT=p_w[pbuf][:, jj * 128:(jj + 1) * 128],
                                rhs=vap, start=(jj == 0), stop=(jj == nkb - 1)),
                                reads=vres + [("pw", pbuf)], writes=[("ppv", slot)])
                        ob = (lbq) % 2
                        if kind == "A":
                            S.add("dve", lambda e: e.reciprocal(out=rinv[rv][:, 0:1], in_=ppv[:, slot, 64:65]),
                                  reads=[("ppv", slot)], writes=[("rinv", rv)])
                        else:
                            hglob = pair_idx * 2 + hh
                            S.add("dve", lambda e: e.tensor_tensor(
                                out=rinv[rv][:, 0:1], in0=ppv[:, slot, 64:65],
                                in1=esink[:, hglob:hglob + 1], op=ALU.add),
                                reads=[("ppv", slot), "esink"], writes=[("rinv", rv)])
                            S.add("dve", lambda e: e.reciprocal(out=rinv[rv][:, 0:1], in_=rinv[rv][:, 0:1]),
                                  reads=[("rinv", rv)], writes=[("rinv", rv)])
                        S.add("act", lambda e: e.activation(
                            out=o_tm[ob][:, hh * 64:(hh + 1) * 64], in_=ppv[:, slot, 0:64], func=AF.Copy,
                            scale=rinv[rv][:, 0:1]),
                            reads=[("ppv", slot), ("rinv", rv)], writes=[("otm", ob, hh)])
                        state["slot"] = slot

                    def tail(lbq=lbq, hh=hh, state=state):
                        if hh != 1:
                            return
                        slot = state["slot"]
                        ob = (lbq) % 2
                        S.add("pe", lambda e: e.transpose(out=ppv[:, slot, 256:384], in_=o_tm[ob][:],
                                                          identity=ident[:]),
                              reads=[("otm", ob, 0), ("otm", ob, 1), "ident"], writes=[("ppv", slot)])
                        S.add("dve", lambda e: e.tensor_tensor(
                            out=oT[:, oc, lbq * 128:(lbq + 1) * 128], in0=ppv[:, slot, 256:384],
                            in1=gT_w[wb][:, lbq * 128:(lbq + 1) * 128], op=ALU.mult),
                            reads=[("ppv", slot), ("gTw", wb)], writes=[("oT", lbq)])

                    units.append((front, back, tail))
            return units

        def run_interleaved(units, fillers, LA=2, TD=1):
            nf = len(fillers)
            nu = len(units)
            fi = 0
            nslot = nu + LA + TD
            for u in range(nslot):
                if u < nu:
                    units[u][0]()
                target = (nf * (u + 1)) // nslot if nu > 0 else nf
                while fi < target:
                    fillers[fi]()
                    fi += 1
                if LA <= u < nu + LA:
                    units[u - LA][1]()
                if u >= LA + TD:
                    units[u - LA - TD][2]()
            while fi < nf:
                fillers[fi]()
                fi += 1

        for st, blocks in enumerate(ST_BLOCKS):
            nb = len(blocks)
            ncols = nb * 128
            if st == 0:
                qblocks = [4, 5, 6, 7]
                sample_lb = 7
                carry_from = [3, 4, 5, 6]
                kvout = {7: ("s", 0)}
            else:
                qblocks = list(range(nb))
                sample_lb = -1
                carry_from = [4, 5, 6, 7] if st < n_st - 1 else None
                kvout = {}
                if st == n_st - 1:
                    for lb, gb in enumerate(blocks):
                        if gb >= 32:
                            kvout[lb] = ("p", gb - 32)
            q0, q1 = qblocks[0] * 128, (qblocks[-1] + 1) * 128

            steps = [("A", p) for p in range(8)] + [("Bkv", 0)] + [("Bq", i) for i in range(8)]

            def weight_parts(kind, idx):
                if kind == "A":
                    return [(0, 128, w_in_v, OFF_QA + idx * 128), (128, 128, w_in_v, OFF_KA + idx * 128),
                            (256, 128, w_in_v, OFF_VA + idx * 128), (384, 128, w_in_v, OFF_GA + idx * 128)]
                if kind == "Bkv":
                    return [(0, 256, w_in_v, OFF_KB), (256, 256, w_in_v, OFF_VB)]
                if kind == "Bq":
                    return [(0, 128, w_in_v, OFF_QB + idx * 128), (128, 128, w_in_v, OFF_GB + idx * 128)]
                return [(0, 512, w_out_v, idx * 512)]

            def tm_proj(slot, wc0, wn, lb, evac):
                def item():
                    pres = ["pv"]
                    o0 = 0
                    for kc in range(16):
                        S.add("pe", lambda e, kc=kc: e.matmul(
                            out=pv[:, o0:o0 + wn], lhsT=hT[:, kc, lb * 128:(lb + 1) * 128],
                            rhs=wsl[slot][:, kc, wc0:wc0 + wn], start=(kc == 0), stop=(kc == 15)),
                            reads=[("wsl", slot), ("hT", lb)], writes=pres)
                    evac(o0, pres)
                return item

            def kv_store(kind, lb, st_buf, which, col0, ncol_out):
                tag, off = kvout[lb]
                if tag == "p":
                    if kind == "A":
                        dst = kvp_a[which, off * 128:(off + 1) * 128, col0:col0 + ncol_out]
                    else:
                        dst = kvp_b[which, 0:128, col0:col0 + ncol_out]
                    srcp = slice(0, 128)
                else:
                    base = 448 if kind == "A" else 64
                    dst = (kvs_a if kind == "A" else kvs_b)[which, base:base + 64, col0:col0 + ncol_out]
                    srcp = slice(0, 64)
                return dst, srcp

            def inproj_items(step_i, kind, idx, slot, wb):
                items = []
                if kind == "A":
                    def evac_k(b, n0, nn):
                        S.add("act", lambda e: e.activation(out=kT_w[wb][:, n0:n0 + nn], in_=pz[:, b, 0:nn], func=AF.Copy),
                              reads=[("pz", b)], writes=[("kTw", wb)])
                    def evac_q(b, n0, nn):
                        S.add("dve", lambda e: e.tensor_scalar(out=qT_w[wb][:, n0:n0 + nn], in0=pz[:, b, 0:nn],
                                                               scalar1=0.125, scalar2=None, op0=ALU.mult),
                              reads=[("pz", b)], writes=[("qTw", wb)])
                    def evac_g(b, n0, nn):
                        xb = xr_ctr[0] % 3
                        xr_ctr[0] += 1
                        S.add("act", lambda e: e.activation(out=xr[xb][:, 0:nn], in_=pz[:, b, 0:nn], func=AF.Tanh, scale=0.5),
                              reads=[("pz", b)], writes=[("xr", xb)])
                        S.add("dve", lambda e: e.scalar_tensor_tensor(out=gT_w[wb][:, n0:n0 + nn], in0=xr[xb][:, 0:nn], scalar=1.0,
                                                                      in1=pz[:, b, 0:nn], op0=ALU.add, op1=ALU.mult),
                              reads=[("pz", b), ("xr", xb)], writes=[("gTw", wb)])
                    items += fm_proj(slot, 128, 0, ncols, evac_k, hreads_blocks)
                    items += fm_proj(slot, 0, q0, q1, evac_q, hreads_blocks)
                    for lb in range(nb):
                        if lb in kvout:
                            def evac(o0, pres, lb=lb):
                                sbf = kvst_ctr[0] % 2
                                kvst_ctr[0] += 1
                                S.add("act", lambda e: e.activation(
                                    out=v_w[wb][:, lb, :, 0:64], in_=pv[:, o0 + 128:o0 + 256].rearrange("p (h d) -> p h d", d=64),
                                    func=AF.Copy),
                                    reads=pres + [("vw1", wb)], writes=[("vw", wb, lb)])
                                S.add("act", lambda e: e.activation(out=kvst[sbf][:, 0:256], in_=pv[:, o0:o0 + 256], func=AF.Copy),
                                      reads=pres, writes=[("kvst", sbf)])
                                for which in range(2):
                                    dst, srcp = kv_store("A", lb, sbf, which, idx * 128, 128)
                                    S.add("sp", lambda e, dst=dst, srcp=srcp, which=which: e.dma_start(
                                        out=dst, in_=kvst[sbf][srcp, which * 128:(which + 1) * 128]),
                                        reads=[("kvst", sbf)], dma=("kvo", sbf))
                            items.append(tm_proj(slot, 128, 256, lb, evac))
                    def evac_vT(b, n0, nn):
                        S.add("act", lambda e: e.activation(out=vT_w[:, n0:n0 + nn], in_=pz[:, b, 0:nn], func=AF.Copy),
                              reads=[("pz", b)], writes=[("vTw", n0)])
                    items += fm_proj(slot, 256, 0, ncols, evac_vT, hreads_blocks)
                    pv_bf = pv[:].bitcast(BF16)
                    nonkv = [lb for lb in range(nb) if lb not in kvout]
                    for i0 in range(0, len(nonkv), 2):
                        batch = nonkv[i0:i0 + 2]

                        def vt_item(batch=batch):
                            for bi, lb in enumerate(batch):
                                S.add("pe", lambda e, bi=bi, lb=lb: e.transpose(
                                    out=pv_bf[:, bi * 128:(bi + 1) * 128], in_=vT_w[:, lb * 128:(lb + 1) * 128],
                                    identity=ident_bf[:]),
                                    reads=[("vTw", (lb * 128) // 512 * 512), "ident_bf"], writes=["pv"])
                            consecutive = all(batch[i] + 1 == batch[i + 1] for i in range(len(batch) - 1))
                            if consecutive:
                                nbb = len(batch)
                                S.add("act", lambda e: e.activation(
                                    out=v_w[wb][:, batch[0]:batch[0] + nbb, :, 0:64],
                                    in_=pv_bf[:, 0:nbb * 128].rearrange("p (b h d) -> p b h d", b=nbb, d=64),
                                    func=AF.Copy),
                                    reads=["pv", ("vw1", wb)], writes=[("vw", wb, lb) for lb in batch])
                            else:
                                for bi, lb in enumerate(batch):
                                    S.add("act", lambda e, bi=bi, lb=lb: e.activation(
                                        out=v_w[wb][:, lb, :, 0:64],
                                        in_=pv_bf[:, bi * 128:(bi + 1) * 128].rearrange("p (h d) -> p h d", d=64),
                                        func=AF.Copy),
                                        reads=["pv", ("vw1", wb)], writes=[("vw", wb, lb)])
                        items.append(vt_item)
                    items += fm_proj(slot, 384, q0, q1, evac_g, hreads_blocks)
                elif kind == "Bkv":
                    for ch in range(2):
                        def evac_k(b, n0, nn, ch=ch):
                            S.add("act", lambda e: e.activation(out=kTb_w[:, ch, n0:n0 + nn], in_=pz[:, b, 0:nn], func=AF.Copy),
                                  reads=[("pz", b)], writes=[("kTbw0", ch, n0), "kTbw"])
                        items += fm_proj(slot, ch * 128, 0, ncols, evac_k, hreads_blocks)
                    for ch in range(2):
                        for (n0, nn) in groups_of(0, ncols):
                            def item(ch=ch, n0=n0, nn=nn):
                                b = pz_ctr[0] % 2
                                pz_ctr[0] += 1
                                S.add("pe", lambda e: e.matmul(out=pz[:, b, 0:nn], lhsT=perm[:],
                                                               rhs=kTb_w[:, ch, n0:n0 + nn], start=True, stop=True),
                                      reads=[("kTbw0", ch, n0), "perm"], writes=[("pz", b)])
                                S.add("dve", lambda e: e.tensor_copy(out=kTb_w[:, 2 + ch, n0:n0 + nn], in_=pz[:, b, 0:nn]),
                                      reads=[("pz", b)], writes=[("kTbw1", ch, n0), "kTbw"])
                            items.append(item)
                    for lb in range(nb):
                        if lb in kvout and (kvout[lb][0] == "s" or kvout[lb][1] == 3):
                            def evac(o0, pres, lb=lb):
                                sbf = kvst_ctr[0] % 2
                                kvst_ctr[0] += 1
                                S.add("act", lambda e: e.activation(
                                    out=vb_w[:, lb, :, 0:64], in_=pv[:, 256:512].rearrange("p (h d) -> p h d", d=64),
                                    func=AF.Copy),
                                    reads=pres + ["vbw1"], writes=[("vbw", lb)])
                                S.add("act", lambda e: e.activation(out=kvst[sbf][:, 0:512], in_=pv[:, 0:512], func=AF.Copy),
                                      reads=pres, writes=[("kvst", sbf)])
                                for which in range(2):
                                    dst, srcp = kv_store("B", lb, sbf, which, 0, 256)
                                    S.add("sp", lambda e, dst=dst, srcp=srcp, which=which: e.dma_start(
                                        out=dst, in_=kvst[sbf][srcp, which * 256:(which + 1) * 256]),
                                        reads=[("kvst", sbf)], dma=("kvo", sbf))
                            items.append(tm_proj(slot, 0, 512, lb, evac))
                        else:
                            def evac(o0, pres, lb=lb):
                                S.add("act", lambda e: e.activation(
                                    out=vb_w[:, lb, :, 0:64], in_=pv[:, o0:o0 + 256].rearrange("p (h d) -> p h d", d=64),
                                    func=AF.Copy),
                                    reads=pres + ["vbw1"], writes=[("vbw", lb)])
                            items.append(tm_proj(slot, 256, 256, lb, evac))
                else:
                    def evac_q(b, n0, nn):
                        S.add("dve", lambda e: e.tensor_scalar(out=qT_w[wb][:, n0:n0 + nn], in0=pz[:, b, 0:nn],
                                                               scalar1=0.125, scalar2=None, op0=ALU.mult),
                              reads=[("pz", b)], writes=[("qTw", wb)])
                    def evac_g(b, n0, nn):
                        xb = xr_ctr[0] % 3
                        xr_ctr[0] += 1
                        S.add("act", lambda e: e.activation(out=xr[xb][:, 0:nn], in_=pz[:, b, 0:nn], func=AF.Tanh, scale=0.5),
                              reads=[("pz", b)], writes=[("xr", xb)])
                        S.add("dve", lambda e: e.scalar_tensor_tensor(out=gT_w[wb][:, n0:n0 + nn], in0=xr[xb][:, 0:nn], scalar=1.0,
                                                                      in1=pz[:, b, 0:nn], op0=ALU.add, op1=ALU.mult),
                              reads=[("pz", b), ("xr", xb)], writes=[("gTw", wb)])
                    items += fm_proj(slot, 0, q0, q1, evac_q, hreads_blocks)
                    items += fm_proj(slot, 128, q0, q1, evac_g, hreads_blocks)
                return items

            def bias_load(kind, idx):
                if kind == "Bkv":
                    return None
                bt = bias_ctr[kind] % 2
                bias_ctr[kind] += 1
                if kind == "A":
                    tile = bA_w[bt]
                    S.add("sp", lambda e: e.dma_start(out=tile[:], in_=biasA_d[2 * idx:2 * idx + 2].rearrange("h p c -> p h c")),
                          reads=[("biasA_d", 2 * idx), ("biasA_d", 2 * idx + 1)],
                          writes=[("bias", "A", id(tile))], dma=("ba", bt))
                    return tile, idx
                tile = bB_w[bt]
                S.add("sp", lambda e: e.dma_start(out=tile[:], in_=alibi_d[:, 2 * idx:2 * idx + 2, :, :]),
                      writes=[("bias", "Bq", id(tile))], dma=("bb", bt))
                return tile, 8 + idx

            def attn_for(step_i, kind, idx, wb, bl):
                if kind == "Bkv":
                    return [], None
                tile, oc = bl
                units = attention_units(st, wb, "A" if kind == "A" else "Bq", idx, qblocks, sample_lb, tile, oc)
                return units, None

            def carry_update(kind, idx, wb):
                if carry_from is None:
                    return
                c0 = carry_from[0] * 128
                if kind == "A":
                    S.add("dve", lambda e: e.tensor_copy(out=kTc[:, idx, :], in_=kT_w[wb][:, c0:c0 + 512]),
                          reads=[("kTw", wb)], writes=[("kTc", idx)])
                    S.add("dve", lambda e: e.tensor_copy(out=vca[:, :, 2 * idx:2 * idx + 2, :],
                                                         in_=v_w[wb][:, carry_from[0]:carry_from[0] + 4, :, :]),
                          reads=[("vw", wb, lb) for lb in carry_from] + [("vw1", wb)], writes=[("vca_c", idx)])

            def carry_update_b():
                if carry_from is None:
                    return
                lbl = carry_from[-1]
                S.add("dve", lambda e: e.tensor_copy(out=kTc[:, 8:12, 384:512], in_=kTb_w[:, :, lbl * 128:(lbl + 1) * 128]),
                      reads=["kTbw"], writes=[("kTc", 8)])
                S.add("dve", lambda e: e.tensor_copy(out=vcb[:], in_=vb_w[:, lbl, :, :]),
                      reads=[("vbw", lbl), "vbw1"], writes=["vcb_c"])

            slots = {}
            slots[0] = load_weights(weight_parts(*steps[0]))
            prev_units = []
            prev_fin = None
            for si, (kind, idx) in enumerate(steps):
                gstep = step_ctr[0]
                step_ctr[0] += 1
                wb = gstep % 2
                slot = slots[si]
                if si + 1 < len(steps):
                    slots[si + 1] = load_weights(weight_parts(*steps[si + 1]))
                else:
                    slots[si + 1] = load_weights(weight_parts("C", 0))
                bl = bias_load(kind, idx)
                items = inproj_items(si, kind, idx, slot, wb)
                dwork = pending_D.pop(0) if pending_D else None
                if dwork:
                    dwork[0]()
                if interleave:
                    run_interleaved(prev_units, items)
                    if prev_fin:
                        prev_fin()
                else:
                    for it in items:
                        it()
                if dwork:
                    dwork[1]()
                if kind == "Bkv":
                    S.add("dve", lambda e: e.memset(stat[:, 63:64], 0.0),
                          reads=[("kTbw0", ch, n0) for ch in range(2) for (n0, nn) in groups_of(0, ncols)] +
                                [("kTbw1", ch, n0) for ch in range(2) for (n0, nn) in groups_of(0, ncols)],
                          writes=["kTbw"])
                units, _ = attn_for(si, kind, idx, wb, bl)
                fin = (lambda kind=kind, idx=idx, wb=wb: carry_update(kind, idx, wb)) if kind == "A" else None
                if interleave:
                    prev_units, prev_fin = units, fin
                else:
                    run_interleaved(units, [])
                    if fin:
                        fin()
                ckpt(10 + st * 100 + si)
            if interleave:
                run_interleaved(prev_units, [])
                if prev_fin:
                    prev_fin()
                prev_units, prev_fin = [], None
            carry_update_b()
            next_blocks = ST_BLOCKS[st + 1] if st + 1 < n_st else []
            a_parts = [phase_A_parts(gb, lb, alt=(lb % 2 == 1)) for lb, gb in enumerate(next_blocks)]
            a_pieces = []
            for k, (pdma, pcomp, groups) in enumerate(a_parts):
                for gi, g in enumerate(groups):
                    pre = None
                    if k + 1 < len(a_parts):
                        if gi == 0:
                            pre = a_parts[k + 1][0]
                        elif gi == 1:
                            pre = a_parts[k + 1][1]
                    a_pieces.append((pre, g))
            if a_parts:
                a_parts[0][0]()
                a_parts[0][1]()
            c_slots = {0: slots[len(steps)]}
            cl = []
            for j in range(4):
                for bi, lbq in enumerate(qblocks):
                    gb = blocks[lbq]
                    st8 = {}

                    def c_load(gb=gb, j=j, st8=st8):
                        xb = xr_ctr[0] % 3
                        xr_ctr[0] += 1
                        st8["xb"] = xb
                        S.add("sp", lambda e: e.dma_start(out=xr[xb][:], in_=xin[gb * 128:(gb + 1) * 128, j * 512:(j + 1) * 512]),
                              writes=[("xr", xb)], dma=("xr", xb))

                    def c_main(lbq=lbq, gb=gb, j=j, bi=bi, st8=st8):
                        if bi == 0 and j + 1 < 4:
                            c_slots[j + 1] = load_weights(weight_parts("C", j + 1))
                        slot = c_slots[j]
                        xb = st8["xb"]
                        b = pz_ctr[0] % 2
                        pz_ctr[0] += 1
                        npart = 64 if gb == SAMPLE_BLK else 128
                        yrow = 4096 if gb == SAMPLE_BLK else (gb - 4) * 128
                        for kc in range(16):
                            S.add("pe", lambda e, kc=kc: e.matmul(
                                out=pz[:, b, :], lhsT=oT[:, kc, lbq * 128:(lbq + 1) * 128], rhs=wsl[slot][:, kc, :],
                                start=(kc == 0), stop=(kc == 15)),
                                reads=[("wsl", slot), ("oT", lbq)], writes=[("pz", b)])
                        S.add("dve", lambda e: e.tensor_tensor(out=xr[xb][:], in0=pz[:, b, :], in1=xr[xb][:], op=ALU.add),
                              reads=[("pz", b), ("xr", xb)], writes=[("xr", xb)])
                        S.add("act", lambda e: e.activation(out=junk[:], in_=xr[xb][:], func=AF.Square,
                                                            accum_out=ssC[:, bi, j:j + 1]),
                              reads=[("xr", xb)], writes=["junk", ("ssC", bi, j)])
                        S.add("act", lambda e: e.dma_start(out=y_d[yrow:yrow + npart, j * 512:(j + 1) * 512],
                                                           in_=xr[xb][0:npart, :]),
                              reads=[("xr", xb)], writes=[("yd", gb, j)], dma=("rs", xb))
                    cl.append((c_load, c_main))
            nci = len(cl)
            na = len(a_pieces)
            for i in range(min(2, nci)):
                cl[i][0]()
            adone = 0
            for i in range(nci):
                if i + 2 < nci:
                    cl[i + 2][0]()
                cl[i][1]()
                tgt = (na * (i + 1)) // nci
                while adone < tgt:
                    pre, g = a_pieces[adone]
                    if pre:
                        pre()
                    g()
                    adone += 1
            while adone < na:
                pre, g = a_pieces[adone]
                if pre:
                    pre()
                g()
                adone += 1
            ckpt(10 + st * 100 + 50)
            nq = len(qblocks)
            S.add("dve", lambda e: e.tensor_reduce(out=stat[:, 48:48 + nq], in_=ssC[:, 0:nq, :], axis=AX.X, op=ALU.add),
                  reads=[("ssC", bi, j) for bi in range(nq) for j in range(4)], writes=["stD"])
            S.add("dve", lambda e: e.tensor_scalar(out=stat[:, 48:48 + nq], in0=stat[:, 48:48 + nq], scalar1=1.0 / D,
                                                   scalar2=EPS, op0=ALU.mult, op1=ALU.add),
                  reads=["stD"], writes=["stD"])
            S.add("act", lambda e: e.activation(out=stat[:, 48:48 + nq], in_=stat[:, 48:48 + nq], func=AF.Sqrt),
                  reads=["stD"], writes=["stD"])
            S.add("dve", lambda e: e.reciprocal(out=stat[:, 48:48 + nq], in_=stat[:, 48:48 + nq]),
                  reads=["stD"], writes=["stD"])
            for bi, lbq in enumerate(qblocks):
                gb = blocks[lbq]
                dst = {}

                def d_load(bi=bi, gb=gb, dst=dst):
                    npart = 64 if gb == SAMPLE_BLK else 128
                    yrow = 4096 if gb == SAMPLE_BLK else (gb - 4) * 128
                    dst["cr"] = 48 + bi
                    S.add("sp", lambda e: e.dma_start(out=rD[0:npart, :], in_=y_d[yrow:yrow + npart, :]),
                          reads=[("yd", gb, j) for j in range(4)], writes=["rD"], dma="dl")

                def d_rest(bi=bi, gb=gb, dst=dst):
                    npart = 64 if gb == SAMPLE_BLK else 128
                    yrow = 4096 if gb == SAMPLE_BLK else (gb - 4) * 128
                    cr = dst["cr"]
                    S.add("dve", lambda e: e.scalar_tensor_tensor(out=rD[:], in0=rD[:], scalar=stat[:, cr:cr + 1],
                                                                  in1=gfin[:], op0=ALU.mult, op1=ALU.mult),
                          reads=["rD", "stD", "gfin"], writes=["rD"])
                    S.add("sp", lambda e: e.dma_start(out=y_d[yrow:yrow + npart, :], in_=rD[0:npart, :]),
                          reads=["rD"], writes=[("yd", gb, 0)], dma="ds")
                pending_D.append((d_load, d_rest))
            if st == n_st - 1:
                while pending_D:
                    dl, dr = pending_D.pop(0)
                    dl()
                    dr()
            if st == 0:
                for i in range(2):
                    S.add("dve", lambda e, i=i: e.memset(v_w[i][:, 0:4, :, 64:66], 2.0),
                          reads=[("vw", i, lb) for lb in range(4)], writes=[("vw1", i)])
                S.add("dve", lambda e: e.memset(vb_w[:, 0:4, :, 64:66], 2.0),
                      reads=[("vbw", lb) for lb in range(4)], writes=["vbw1"])

        S.emit(nc)
    return nc


_CACHE = {}


def _host_consts():
    p = np.arange(128)[:, None, None]
    jj = np.arange(5)[None, :, None]
    f = np.arange(128)[None, None, :]
    maskA = np.zeros((128, 5, 128), np.float32)
    maskA[(jj == 0) & (p < 64) & (f >= 64) | (jj == 4) & (p >= 64) & (f < 64)] = NEG
    slopes = (2.0 ** (-8.0 * np.arange(1, 17, dtype=np.float32) / 16)).astype(np.float32)
    j2 = np.arange(2)[None, :, None]
    d = (128 + f - 128 * j2 - p).astype(np.float32)
    al = -slopes[None, :, None, None] * np.abs(d)[:, None, :, :]
    mB = ((j2 == 0) & (p < 64) & (f >= 64)) | ((j2 == 1) & (p >= 64) & (f < 64))
    al = np.where(mB[:, None, :, :], np.float32(NEG), al).astype(np.float32)
    ident = np.eye(128, dtype=np.float32)
    perm = np.roll(np.eye(128, dtype=np.float32), 64, axis=1)
    return maskA, np.ascontiguousarray(al), ident, np.ascontiguousarray(perm)


def kernel(x_prompt, x_sample, cache_a_k, cache_a_v, cache_b_k, cache_b_v,
           norm_in, w_in, rel_bias_a, sinks_b, w_out, norm_final):
    f32 = np.float32
    x_prompt = np.asarray(x_prompt, f32)
    x_sample = np.asarray(x_sample, f32)
    if "nc" not in _CACHE:
        _CACHE["nc"] = build_program()
    nc = _CACHE["nc"]
    maskA, alibi, ident, perm = _host_consts()
    w_in2 = np.ascontiguousarray(np.asarray(w_in, f32)[0])
    w_out2 = np.ascontiguousarray(np.asarray(w_out, f32)[0])
    g_inT = np.ascontiguousarray(np.asarray(norm_in, f32)[0].reshape(16, 128).T)
    gfin = np.ascontiguousarray(np.broadcast_to(np.asarray(norm_final, f32)[None, :], (128, D)))
    relb = np.ascontiguousarray(np.asarray(rel_bias_a, f32)[0])
    sinks = np.ascontiguousarray(np.broadcast_to(np.asarray(sinks_b, f32)[0][None, :], (128, 16)))
    in_maps = []
    for c in range(8):
        b, qd = c // 4, c % 4
        xin = np.zeros((XROWS, D), f32)
        if qd > 0:
            xin[0:512] = x_prompt[b, qd * 4096 - 512:qd * 4096]
        xin[512:512 + 4096] = x_prompt[b, qd * 4096:(qd + 1) * 4096]
        xin[SAMPLE_BLK * 128:SAMPLE_BLK * 128 + 64] = x_sample[c]
        hv = np.full((128, 4), 2.0 if qd > 0 else 0.0, f32)
        in_maps.append({
            "xin": xin, "w_in": w_in2, "w_out": w_out2, "g_inT": g_inT, "gfin": gfin, "relb": relb,
            "sinks": sinks,
            "cak": np.ascontiguousarray(np.asarray(cache_a_k, f32)[0, c].reshape(512, 1024)),
            "cav": np.ascontiguousarray(np.asarray(cache_a_v, f32)[0, c].reshape(512, 1024)),
            "cbk": np.ascontiguousarray(np.asarray(cache_b_k, f32)[0, c].reshape(128, 256)),
            "cbv": np.ascontiguousarray(np.asarray(cache_b_v, f32)[0, c].reshape(128, 256)),
            "hvalid": hv, "alibi": alibi, "maskA": maskA, "ident": ident, "perm": perm,
            "antiI": np.ascontiguousarray(ident[::-1]),
        })
    if _CACHE.get("only_maps"):
        return in_maps
    res = run_bass_kernel_spmd(nc, in_maps, core_ids=list(range(8)))
    R = res.results
    y_prompt = np.empty((2, 16384, D), f32)
    y_sample = np.empty((8, 64, D), f32)
    for c in range(8):
        b, qd = c // 4, c % 4
        y_prompt[b, qd * 4096:(qd + 1) * 4096] = R[c]["y"][0:4096]
        y_sample[c] = R[c]["y"][4096:4160]
    akp = np.stack([R[3]["kvp_a"][0], R[7]["kvp_a"][0]]).reshape(1, 2, 512, 16, 64)
    avp = np.stack([R[3]["kvp_a"][1], R[7]["kvp_a"][1]]).reshape(1, 2, 512, 16, 64)
    bkp = np.stack([R[3]["kvp_b"][0], R[7]["kvp_b"][0]]).reshape(1, 2, 128, 4, 64)
    bvp = np.stack([R[3]["kvp_b"][1], R[7]["kvp_b"][1]]).reshape(1, 2, 128, 4, 64)
    aks = np.stack([R[c]["kvs_a"][0] for c in range(8)]).reshape(1, 8, 512, 16, 64)
    avs = np.stack([R[c]["kvs_a"][1] for c in range(8)]).reshape(1, 8, 512, 16, 64)
    bks = np.stack([R[c]["kvs_b"][0] for c in range(8)]).reshape(1, 8, 128, 4, 64)
    bvs = np.stack([R[c]["kvs_b"][1] for c in range(8)]).reshape(1, 8, 128, 4, 64)
    return (y_prompt, y_sample, akp, avp, bkp, bvp, aks, avs, bks, bvs)
```

```python
from contextlib import ExitStack
import numpy as np
import concourse.bass as bass
import concourse.mybir as mybir
from concourse.bass_utils import run_bass_kernel_spmd

F32 = mybir.dt.float32
BF16 = mybir.dt.bfloat16
AF = mybir.ActivationFunctionType
ALU = mybir.AluOpType
AX = mybir.AxisListType

ENG_ATTR = {"pe": "tensor", "act": "scalar", "dve": "vector", "pool": "gpsimd", "sp": "sync"}


class _Op:
    __slots__ = ("eng", "fn", "dma", "idx", "deps", "dmawaits", "milestone", "cnt")


class _Rec:
    def __getattr__(self, name):
        def call(*a, **k):
            self.__dict__["rec"] = (name, a, k)
            return None
        return call


class Sched:
    def __init__(self):
        self.ops = []
        self.last_write = {}
        self.readers = {}
        self.dma_cnt = {}
        self.dma_waiters = {}
        self.frozen = False
        import os as _os
        self.maxops = int(_os.environ.get("KMAXOPS", "100000000"))

    def add(self, eng, fn, reads=(), writes=(), dma=None):
        if self.frozen:
            return -1
        i = len(self.ops)
        if i >= self.maxops:
            self.frozen = True
            return -1
        op = _Op()
        rec = _Rec()
        fn(rec)
        op.eng, op.fn, op.dma, op.idx = eng, rec.__dict__["rec"], dma, i
        op.milestone = False
        op.cnt = 0
        deps = {}
        for r in reads:
            j = self.last_write.get(r)
            if j is not None:
                deps[j] = "raw"
        for w in writes:
            j = self.last_write.get(w)
            if j is not None and j not in deps:
                deps[j] = "waw"
            for k in self.readers.get(w, ()):
                if k not in deps:
                    deps[k] = "war"
        if dma is not None:
            for k in self.dma_waiters.get(dma, ()):
                if k not in deps:
                    deps[k] = "war"
            self.dma_waiters[dma] = []
        op.deps = []
        op.dmawaits = []
        latest = {}
        for j, kind in deps.items():
            oj = self.ops[j]
            if oj.dma is not None:
                op.dmawaits.append((oj.dma, self.dma_cnt[oj.dma]))
                self.dma_waiters.setdefault(oj.dma, []).append(i)
            else:
                if oj.eng == eng and dma is None and eng == "pe":
                    continue
                if latest.get(oj.eng, -1) < j:
                    latest[oj.eng] = j
        for j in latest.values():
            self.ops[j].milestone = True
            op.deps.append(j)
        if dma is not None:
            self.dma_cnt[dma] = self.dma_cnt.get(dma, 0) + 16
        for r in reads:
            self.readers.setdefault(r, []).append(i)
        for w in writes:
            self.last_write[w] = i
            self.readers[w] = []
        self.ops.append(op)
        return i

    def emit(self, nc):
        cnt = {e: 0 for e in ENG_ATTR}
        for op in self.ops:
            if op.milestone:
                cnt[op.eng] += 1
                op.cnt = cnt[op.eng]
        with ExitStack() as es:
            sems = {}
            for e in ENG_ATTR:
                if cnt[e] > 0:
                    sems[("eng", e)] = es.enter_context(nc.semaphore("s_" + e))
            for n, k in enumerate(self.dma_cnt):
                sems[("dma", k)] = es.enter_context(nc.semaphore("d_%d" % n))
            block = es.enter_context(nc.Block())
            for eng, attr in ENG_ATTR.items():
                ops_e = [op for op in self.ops if op.eng == eng]

                def body(e, ops_e=ops_e, eng=eng):
                    waited = {}
                    for op in ops_e:
                        need = {}
                        for j in op.deps:
                            oj = self.ops[j]
                            key = ("eng", oj.eng)
                            need[key] = max(need.get(key, 0), oj.cnt)
                        for (k, v) in op.dmawaits:
                            key = ("dma", k)
                            need[key] = max(need.get(key, 0), v)
                        for key, v in need.items():
                            if waited.get(key, 0) < v:
                                e.wait_ge(sems[key], v)
                                waited[key] = v
                        name, a, k = op.fn
                        ins = getattr(e, name)(*a, **k)
                        if op.dma is not None:
                            ins.then_inc(sems[("dma", op.dma)], 16)
                        elif op.milestone:
                            ins.then_inc(sems[("eng", eng)], 1)
                    if eng == "sp":
                        for k, v in self.dma_cnt.items():
                            e.wait_ge(sems[("dma", k)], v)

                getattr(block, attr)(body)
        return cnt


D = 2048
D_IN = 6656
NBLK_OWN = 32
SAMPLE_BLK = 36
XROWS = 37 * 128
YROWS = 4096 + 64
OFF_QA, OFF_KA, OFF_VA, OFF_GA, OFF_QB, OFF_KB, OFF_VB, OFF_GB = 0, 1024, 2048, 3072, 4096, 5120, 5376, 5632
NEG = -30000.0
BATCH_B = False
FM_SPLIT = 16
EPS = 1e-6

ST_BLOCKS = [[0, 1, 2, 3, 4, 5, 6, SAMPLE_BLK], list(range(7, 15)), list(range(15, 23)),
             list(range(23, 31)), list(range(31, 36))]


def build_program(interleave=True, stop=None):
    import os
    if stop is None:
        stop = int(os.environ.get("KSTOP", "999"))
    nc = bass.Bass("TRN2", target_bir_lowering=False)
    din = lambda n, s: nc.dram_tensor(n, s, F32, kind="ExternalInput").ap()
    dout = lambda n, s: nc.dram_tensor(n, s, F32, kind="ExternalOutput").ap()
    xin = din("xin", [XROWS, D])
    w_in = din("w_in", [D, D_IN])
    w_out = din("w_out", [D, D])
    g_inT_d = din("g_inT", [128, 16])
    gfin_d = din("gfin", [128, D])
    relb_d = din("relb", [16, 320])
    sinks_d = din("sinks", [128, 16])
    cak, cav = din("cak", [512, 1024]), din("cav", [512, 1024])
    cbk, cbv = din("cbk", [128, 256]), din("cbv", [128, 256])
    hvalid_d = din("hvalid", [128, 4])
    alibi_d = din("alibi", [128, 16, 2, 128])
    maskA_d = din("maskA", [128, 5, 128])
    ident_d = din("ident", [128, 128])
    perm_d = din("perm", [128, 128])
    antiI_d = din("antiI", [128, 128])
    y_d = dout("y", [YROWS, D])
    kvp_a = dout("kvp_a", [2, 512, 1024])
    kvp_b = dout("kvp_b", [2, 128, 256])
    kvs_a = dout("kvs_a", [2, 512, 1024])
    kvs_b = dout("kvs_b", [2, 128, 256])
    ext_h = nc.dram_tensor("ext_tab", [16, 768], F32, kind="Internal")
    ext_d = ext_h.ap()
    biasA_d = nc.dram_tensor("biasA_s", [16, 128, 640], F32, kind="Internal").ap()

    w_in_v = w_in.rearrange("(kc p) c -> p kc c", p=128)
    w_out_v = w_out.rearrange("(kc p) c -> p kc c", p=128)

    S = Sched()

    def ckpt(level):
        if os.environ.get("KPRINT"):
            print("ckpt", level, "nops", len(S.ops))
        if level >= stop:
            S.frozen = True

    es = ExitStack()
    with es:
        sb = lambda n, s, d: es.enter_context(nc.sbuf_tensor("s_" + n, s, d))
        pt = lambda n, s, d: es.enter_context(nc.psum_tensor("p_" + n, s, d))
        hT = sb("hT", [128, 16, 1024], BF16)
        oT = sb("oT", [128, 16, 1024], BF16)
        wsl = [sb("wsl%d" % i, [128, 16, 512], BF16) for i in range(2)]
        kTc = sb("kTc", [128, 12, 512], BF16)
        vca = sb("vca", [128, 4, 16, 66], BF16)
        vcb = sb("vcb", [128, 4, 66], BF16)
        qT_w = [sb("qTw%d" % i, [128, 1024], BF16) for i in range(2)]
        kT_w = [sb("kTw%d" % i, [128, 1024], BF16) for i in range(2)]
        gT_w = [sb("gTw%d" % i, [128, 1024], BF16) for i in range(2)]
        v_w = [sb("vw%d" % i, [128, 8, 2, 66], BF16) for i in range(2)]
        kTb_w = sb("kTbw", [128, 4, 1024], BF16)
        vT_w = sb("vTw", [128, 1024], BF16)
        ident_bf = sb("ident_bf", [128, 128], BF16)
        vb_w = sb("vbw", [128, 8, 4, 66], BF16)
        t_w = [sb("tw%d" % i, [128, 640], F32) for i in range(2)]
        p_w = [sb("pw%d" % i, [128, 640], BF16) for i in range(3)]
        o_tm = [sb("otm%d" % i, [128, 128], F32) for i in range(2)]
        rinv = [sb("rinv%d" % i, [128, 2], F32) for i in range(4)]
        bA_w = [sb("bAw%d" % i, [128, 2, 640], F32) for i in range(2)]
        bB_w = [sb("bBw%d" % i, [128, 2, 2, 128], F32) for i in range(2)]
        gfin = sb("gfin", [128, D], F32)
        xs = sb("xs", [128, D], F32)
        xs_main = xs
        junk = sb("junk", [128, 512], BF16)
        xr = [sb("xr%d" % i, [128, 512], F32) for i in range(3)]
        rD = sb("rD", [128, D], F32)
        kvst = [sb("kvst%d" % i, [128, 512], F32) for i in range(2)]
        ident = sb("ident", [128, 128], F32)
        perm_f = kvst[1][:, 0:128]
        antiI = kvst[1][:, 128:256]
        perm = sb("perm", [128, 128], BF16)
        g_inT = sb("g_inT", [128, 16], F32)
        esink = sb("esink", [128, 16], F32)
        hvalid = sb("hvalid", [128, 4], F32)
        stat = sb("stat", [128, 64], F32)
        ssC = sb("ssC", [128, 8, 4], F32)
        ext_sb = rD[0:16, 0:768]

        pz = pt("pz", [128, 2, 512], F32)
        pv = pt("pv", [128, 512], F32)
        pst = pt("pst", [128, 1, 1536], F32)
        ppv = pt("ppv", [128, 2, 512], F32)

        maskA = kTb_w[:].rearrange("p a b -> p (a b)").bitcast(F32)[:, 0:640]

        def dma(q, out, in_, reads=(), writes=(), key=None):
            S.add(q, lambda e: e.dma_start(out=out, in_=in_), reads=reads, writes=writes, dma=key)

        dma("sp", ident[:], ident_d, writes=["ident"], key="pro")
        dma("sp", perm_f, perm_d, writes=["perm_f"], key="pro")
        dma("sp", antiI, antiI_d, writes=["antiI"], key="pro")
        dma("sp", g_inT[:], g_inT_d, writes=["g_inT"], key="pro")
        dma("sp", esink[:], sinks_d, writes=["esink"], key="pro")
        dma("sp", hvalid[:], hvalid_d, writes=["hvalid"], key="pro")
        dma("sp", maskA, maskA_d.rearrange("p a b -> p (a b)"), writes=["maskA", "kTbw"], key="pro")
        dma("sp", rD[0:16, 64:384], relb_d, writes=["ext_sb", "rD"], key="pro")
        dma("sp", gfin[:], gfin_d, writes=["gfin"], key="pro")
        S.add("dve", lambda e: e.tensor_copy(out=perm[:], in_=perm_f), reads=["perm_f"], writes=["perm"])
        S.add("dve", lambda e: e.tensor_copy(out=ident_bf[:], in_=ident[:]), reads=["ident"], writes=["ident_bf"])
        S.add("act", lambda e: e.activation(out=esink[:], in_=esink[:], func=AF.Exp),
              reads=["esink"], writes=["esink"])
        S.add("dve", lambda e: e.tensor_scalar(out=esink[:], in0=esink[:], scalar1=2.0, scalar2=None, op0=ALU.mult),
              reads=["esink"], writes=["esink"])
        S.add("dve", lambda e: e.tensor_copy(out=rD[0:16, 0:64], in_=rD[0:16, 64:128]),
              reads=["ext_sb"], writes=["ext_sb0"])
        S.add("dve", lambda e: e.tensor_copy(out=rD[0:16, 384:768],
                                             in_=rD[0:16, 383:384].to_broadcast([16, 384])),
              reads=["ext_sb"], writes=["ext_sb1"])
        dma("sp", ext_d, rD[0:16, 0:768], reads=["ext_sb", "ext_sb0", "ext_sb1", "rD"], writes=["ext_d"], key="pro2")
        def bias_round(r):
            Wv = oT[:].rearrange("p a b -> p (a b)").bitcast(F32)[:, 0:5120].rearrange("p (h u) -> p h u", u=640)
            src = bass.AP(tensor=ext_h, offset=r * 8 * 768, ap=[[1, 128], [768, 8], [1, 640]])
            dma("sp", Wv, src, reads=["ext_d"], writes=["Wall"] + [("oT", lb) for lb in range(8)], key="pb")
            for hh in range(8):
                h = 8 * r + hh
                b = h % 2
                for jj in range(5):
                    S.add("pe", lambda e, jj=jj: e.matmul(
                        out=pst[:, 0, jj * 128:(jj + 1) * 128], lhsT=antiI,
                        rhs=Wv[:, hh, 512 - 128 * jj:640 - 128 * jj], start=True, stop=True),
                        reads=["Wall", "antiI"] + [("oT", lb) for lb in range(8)], writes=[("pst", jj // 4)])
                S.add("dve", lambda e: e.tensor_tensor(out=t_w[b][:], in0=pst[:, 0, 0:640], in1=maskA, op=ALU.add),
                      reads=[("pst", 0), ("pst", 1), "maskA", "kTbw"], writes=[("tw", b)])
                dma("sp", biasA_d[h], t_w[b][:], reads=[("tw", b)], writes=[("biasA_d", h)], key=("pb2", b))

        ckpt(1)
        S.add("dve", lambda e: e.memset(vca[:, :, :, 64:66], 2.0), writes=["vca1"])
        S.add("dve", lambda e: e.memset(vcb[:, :, 64:66], 2.0), writes=["vcb1"])
        S.add("dve", lambda e: e.memset(vb_w[:, :, :, 64:66], 2.0), writes=["vbw1"])
        for i in range(2):
            S.add("dve", lambda e, i=i: e.memset(v_w[i][:, :, :, 64:66], 2.0), writes=[("vw1", i)])
        for lb in range(4):
            for i in range(2):
                S.add("dve", lambda e, i=i, lb=lb: e.tensor_copy(
                    out=v_w[i][:, lb, :, 64:66], in_=hvalid[:, lb:lb + 1].unsqueeze(1).to_broadcast([128, 2, 2])),
                    reads=["hvalid", ("vw1", i)], writes=[("vw1", i)])
            S.add("dve", lambda e, lb=lb: e.tensor_copy(
                out=vb_w[:, lb, :, 64:66], in_=hvalid[:, lb:lb + 1].unsqueeze(1).to_broadcast([128, 4, 2])),
                reads=["hvalid", "vbw1"], writes=["vbw1"])

        def cache_to_carry():
            xs4 = xs[:].rearrange("p (a c) -> p a c", a=4)
            for half in range(2):
                dma("sp", xs4, cak.rearrange("(a p) c -> p a c", p=128)[:, :, half * 512:(half + 1) * 512],
                    writes=["xs"], key="xs")
                for pp in range(4):
                    pair = half * 4 + pp
                    for blk in range(4):
                        S.add("pe", lambda e, pp=pp, blk=blk: e.transpose(
                            out=pst[:, 0, blk * 128:(blk + 1) * 128], in_=xs4[:, blk, pp * 128:(pp + 1) * 128],
                            identity=ident[:]), reads=["xs", "ident"], writes=[("pst", 0)])
                    S.add("dve", lambda e, pair=pair: e.tensor_copy(out=kTc[:, pair, :], in_=pst[:, 0, 0:512]),
                          reads=[("pst", 0)], writes=[("kTc", pair)])
                dma("sp", xs4, cav.rearrange("(a p) c -> p a c", p=128)[:, :, half * 512:(half + 1) * 512],
                    writes=["xs"], key="xs")
                for blk in range(4):
                    S.add("dve", lambda e, blk=blk, half=half: e.tensor_copy(
                        out=vca[:, blk, half * 8:(half + 1) * 8, 0:64],
                        in_=xs4[:, blk, :].rearrange("p (h d) -> p h d", d=64)),
                        reads=["xs", "vca1"], writes=[("vca_c", half * 4 + pp) for pp in range(4)])
            dma("sp", xs[:, 0:256], cbk, writes=["xs"], key="xs")
            dma("sp", xs[:, 256:512], cbv, writes=["xsb"], key="xs")
            for ch in range(2):
                S.add("pe", lambda e, ch=ch: e.transpose(out=pst[:, 0, ch * 128:(ch + 1) * 128],
                                                         in_=xs[:, ch * 128:(ch + 1) * 128], identity=ident[:]),
                      reads=["xs", "ident"], writes=[("pst", 0)])
            S.add("dve", lambda e: e.tensor_copy(out=kTc[:, 8:10, 384:512],
                                                 in_=pst[:, 0, 0:256].rearrange("p (c k) -> p c k", c=2)),
                  reads=[("pst", 0)], writes=[("kTc", 8)])
            for ch in range(2):
                S.add("pe", lambda e, ch=ch: e.matmul(out=pst[:, 0, 512 + ch * 128:512 + (ch + 1) * 128], lhsT=perm[:],
                                                      rhs=kTc[:, 8 + ch, 384:512], start=True, stop=True),
                      reads=[("kTc", 8), "perm"], writes=[("pst", 1)])
            S.add("dve", lambda e: e.tensor_copy(out=kTc[:, 10:12, 384:512],
                                                 in_=pst[:, 0, 512:768].rearrange("p (c k) -> p c k", c=2)),
                  reads=[("pst", 1)], writes=[("kTc", 8)])
            S.add("dve", lambda e: e.tensor_copy(out=vcb[:, :, 0:64],
                                                 in_=xs[:, 256:512].rearrange("p (h d) -> p h d", d=64)),
                  reads=["xsb", "vcb1"], writes=["vcb_c"])
            dma("sp", kvs_a[0, 0:448, :], cak[64:512, :], key="kvroll")
            dma("sp", kvs_a[1, 0:448, :], cav[64:512, :], key="kvroll")
            dma("sp", kvs_b[0, 0:64, :], cbk[64:128, :], key="kvroll")
            dma("sp", kvs_b[1, 0:64, :], cbv[64:128, :], key="kvroll")


        ckpt(2)
        stat_col = [0]

        def new_stat(n=1):
            c = stat_col[0]
            if c + n > 48:
                c = 0
            stat_col[0] = c + n
            return c

        def rstd_from(ss_ap_fn, ss_res, out_res):
            c = new_stat()
            S.add("dve", lambda e: e.tensor_scalar(out=stat[:, c:c + 1], in0=ss_ap_fn(), scalar1=1.0 / D,
                                                   scalar2=EPS, op0=ALU.mult, op1=ALU.add),
                  reads=[ss_res], writes=[("stat", c)])
            S.add("act", lambda e: e.activation(out=stat[:, c:c + 1], in_=stat[:, c:c + 1], func=AF.Sqrt),
                  reads=[("stat", c)], writes=[("stat", c)])
            S.add("dve", lambda e: e.reciprocal(out=stat[:, c:c + 1], in_=stat[:, c:c + 1]),
                  reads=[("stat", c)], writes=[("stat", c), out_res])
            return c

        def phase_A_parts(gb, lb, alt=False):
            xs, xsr, xkey = (rD, "rD", "xs2") if alt else (xs_main, "xs", "xs")

            def prep_dma():
                dma("sp", xs[:], xin[gb * 128:(gb + 1) * 128, :], writes=[xsr], key=xkey)

            def prep():
                c0 = new_stat()
                cfirst = new_stat(4)
                for q4 in range(4):
                    cq = cfirst + q4
                    S.add("act", lambda e, q4=q4, cq=cq: e.activation(
                        out=junk[:], in_=xs[:, q4 * 512:(q4 + 1) * 512], func=AF.Square,
                        accum_out=stat[:, cq:cq + 1]), reads=[xsr], writes=["junk", ("stat", cq)])
                S.add("dve", lambda e: e.tensor_reduce(out=stat[:, c0:c0 + 1], in_=stat[:, cfirst:cfirst + 4],
                                                       axis=AX.X, op=ALU.add),
                      reads=[("stat", cfirst + i) for i in range(4)], writes=[("stat", c0)])
                cr = rstd_from(lambda: stat[:, c0:c0 + 1], ("stat", c0), ("rstdA", lb))
                S.add("act", lambda e: e.activation(out=xs[:], in_=xs[:], func=AF.Copy, scale=stat[:, cr:cr + 1]),
                      reads=[xsr, ("stat", cr)], writes=[xsr])

            def group(grp):
                buf = grp % 2
                for k4 in range(4):
                    kc = grp * 4 + k4
                    S.add("pe", lambda e, kc=kc, k4=k4: e.transpose(
                        out=pst[:, 0, buf * 512 + k4 * 128:buf * 512 + (k4 + 1) * 128], in_=xs[:, kc * 128:(kc + 1) * 128],
                        identity=ident[:]), reads=[xsr, "ident"], writes=[("pst", buf)])
                S.add("dve", lambda e: e.tensor_tensor(
                    out=hT[:, grp * 4:(grp + 1) * 4, lb * 128:(lb + 1) * 128],
                    in0=pst[:, 0, buf * 512:(buf + 1) * 512].rearrange("p (c k) -> p c k", c=4),
                    in1=g_inT[:, grp * 4:(grp + 1) * 4].unsqueeze(2).to_broadcast([128, 4, 128]),
                    op=ALU.mult), reads=[("pst", buf), "g_inT"], writes=[("hT", lb)])

            return prep_dma, prep, [(lambda grp=grp: group(grp)) for grp in range(4)]

        def phase_A_block(gb, lb, alt=False):
            prep_dma, prep, groups = phase_A_parts(gb, lb, alt)
            prep_dma()
            prep()
            for g in groups:
                g()


        wslot_ctr = [0]

        def load_weights(parts):
            s = wslot_ctr[0] % 2
            wslot_ctr[0] += 1
            for (d0, n, view, c0) in parts:
                S.add("pool", lambda e, d0=d0, n=n, view=view, c0=c0, s=s: e.dma_start(
                    out=wsl[s][:, :, d0:d0 + n], in_=view[:, :, c0:c0 + n]),
                    writes=[("wsl", s)], dma=("w", s))
            return s

        pz_ctr = [0]

        def groups_of(c0, c1, g=512):
            out = []
            c = c0
            while c < c1:
                n = min(g, c1 - c)
                out.append((c, n))
                c += n
            return out

        def fm_proj(slot, wc0, c0, c1, evac, hreads):
            items = []
            for (n0, nn) in groups_of(c0, c1):
                gst = {}
                for part in range(FM_SPLIT):
                    def item(n0=n0, nn=nn, part=part, gst=gst):
                        if part == 0:
                            gst["b"] = pz_ctr[0] % 2
                            pz_ctr[0] += 1
                        b = gst["b"]
                        per = 16 // FM_SPLIT
                        for kc in range(part * per, (part + 1) * per):
                            S.add("pe", lambda e, kc=kc, b=b: e.matmul(
                                out=pz[:, b, 0:nn], lhsT=wsl[slot][:, kc, wc0:wc0 + 128], rhs=hT[:, kc, n0:n0 + nn],
                                start=(kc == 0), stop=(kc == 15)),
                                reads=[("wsl", slot)] + hreads(n0, nn), writes=[("pz", b)])
                        if part == FM_SPLIT - 1:
                            evac(b, n0, nn)
                    items.append(item)
            return items

        def hreads_blocks(n0, nn):
            return [("hT", lb) for lb in range(n0 // 128, (n0 + nn + 127) // 128)]

        n_st = len(ST_BLOCKS)
        pm_ctr = [0]
        st_ctr = [0]
        step_ctr = [0]
        kvst_ctr = [0]
        xr_ctr = [0]
        pv_ctr = [0]
        pending_D = []
        bias_ctr = {"A": 0, "Bq": 0}

        bias_round(0)
        for lb, gb in enumerate(ST_BLOCKS[0]):
            phase_A_block(gb, lb, alt=(lb % 2 == 1))
        bias_round(1)
        cache_to_carry()
        ckpt(3)

        def attention_units(st, wb, kind, pair_idx, qblocks, sample_lb, bias_tile, oc):
            units = []
            nkb = 5 if kind == "A" else 2
            ncol = nkb * 128
            if kind != "A" and BATCH_B:
                g = pair_idx // 2
                gch, gh = g // 2, g % 2
                for lbq in qblocks:
                    is_s = (lbq == sample_lb)

                    def kv_b(hh, jj, lbq=lbq, is_s=is_s):
                        lk = lbq - 1 + jj
                        r0 = hh * 64
                        ch = gch if hh == gh else 2 + gch
                        use_carry = False
                        if is_s:
                            if jj == 0:
                                use_carry = True
                            else:
                                lk = lbq
                        elif lk < 0:
                            use_carry = True
                        if use_carry:
                            return (kTc[r0:r0 + 64, 8 + ch, 384:512], vcb[:, g, 0:65], [("kTc", 8)], ["vcb_c"])
                        return (kTb_w[r0:r0 + 64, ch, lk * 128:(lk + 1) * 128], vb_w[:, lk, g, 0:65],
                                ["kTbw"], [("vbw", lk)])

                    state = {}

                    def front(lbq=lbq, kv_b=kv_b, state=state):
                        u = st_ctr[0]
                        st_ctr[0] += 1
                        sbuf = u % 2
                        pbuf = u % 3
                        state["sbuf"], state["pbuf"] = sbuf, pbuf
                        pc0 = sbuf * 512
                        for hh in range(2):
                            for jj in range(2):
                                kap, vap, kres, vres = kv_b(hh, jj)
                                oc0 = pc0 + (hh * 2 + jj) * 128
                                S.add("pe", lambda e, kap=kap, oc0=oc0, hh=hh: e.matmul(
                                    out=pst[:, 0, oc0:oc0 + 128], lhsT=kap,
                                    rhs=qT_w[wb][hh * 64:(hh + 1) * 64, lbq * 128:(lbq + 1) * 128],
                                    start=True, stop=True), reads=kres + [("qTw", wb)], writes=[("pst", sbuf)])
                        bap = bias_tile[:].rearrange("p h a b -> p (h a b)")
                        S.add("dve", lambda e: e.tensor_tensor(out=t_w[sbuf][:, 0:512], in0=pst[:, 0, pc0:pc0 + 512],
                                                               in1=bap, op=ALU.add),
                              reads=[("pst", sbuf), ("bias", kind, id(bias_tile))], writes=[("tw", sbuf), ("tw5", sbuf)])
                        S.add("act", lambda e: e.activation(out=p_w[pbuf][:, 0:512], in_=t_w[sbuf][:, 0:512], func=AF.Exp),
                              reads=[("tw", sbuf), ("tw5", sbuf)], writes=[("pw", pbuf)])

                    def back(lbq=lbq, kv_b=kv_b, state=state):
                        pbuf = state["pbuf"]
                        slot = pm_ctr[0] % 2
                        pm_ctr[0] += 1
                        rv = slot
                        state["slot"] = slot
                        for hh in range(2):
                            for jj in range(2):
                                kap, vap, kres, vres = kv_b(hh, jj)
                                S.add("pe", lambda e, jj=jj, hh=hh, vap=vap: e.matmul(
                                    out=ppv[:, slot, hh * 66:hh * 66 + 65],
                                    lhsT=p_w[pbuf][:, (hh * 2 + jj) * 128:(hh * 2 + jj + 1) * 128],
                                    rhs=vap, start=(jj == 0), stop=(jj == 1)),
                                    reads=vres + [("pw", pbuf)], writes=[("ppv", slot)])
                        ob = lbq % 2
                        den = ppv[:, slot, 0:132].rearrange("p (h c) -> p h c", c=66)[:, :, 64]
                        S.add("dve", lambda e: e.tensor_tensor(out=rinv[rv][:, 0:2], in0=den,
                                                               in1=esink[:, 2 * pair_idx:2 * pair_idx + 2], op=ALU.add),
                              reads=[("ppv", slot), "esink"], writes=[("rinv", rv)])
                        S.add("dve", lambda e: e.reciprocal(out=rinv[rv][:, 0:2], in_=rinv[rv][:, 0:2]),
                              reads=[("rinv", rv)], writes=[("rinv", rv)])
                        for hh in range(2):
                            S.add("act", lambda e, hh=hh: e.activation(
                                out=o_tm[ob][:, hh * 64:(hh + 1) * 64], in_=ppv[:, slot, hh * 66:hh * 66 + 64],
                                func=AF.Copy, scale=rinv[rv][:, hh:hh + 1]),
                                reads=[("ppv", slot), ("rinv", rv)], writes=[("otm", ob, hh)])

                    def tail(lbq=lbq, state=state):
                        slot = state["slot"]
                        ob = lbq % 2
                        S.add("pe", lambda e: e.transpose(out=ppv[:, slot, 256:384], in_=o_tm[ob][:], identity=ident[:]),
                              reads=[("otm", ob, 0), ("otm", ob, 1), "ident"], writes=[("ppv", slot)])
                        S.add("dve", lambda e: e.tensor_tensor(
                            out=oT[:, oc, lbq * 128:(lbq + 1) * 128], in0=ppv[:, slot, 256:384],
                            in1=gT_w[wb][:, lbq * 128:(lbq + 1) * 128], op=ALU.mult),
                            reads=[("ppv", slot), ("gTw", wb)], writes=[("oT", lbq)])

                    units.append((front, back, tail))
                return units
            for lbq in qblocks:
                is_s = (lbq == sample_lb)
                for hh in range(2):
                    def kv_src(jj, lbq=lbq, hh=hh, is_s=is_s):
                        lk = lbq - (nkb - 1) + jj
                        r0 = hh * 64
                        use_carry = False
                        if is_s:
                            if jj < nkb - 1:
                                use_carry, cblk = True, (4 - (nkb - 1)) + jj
                            else:
                                lk = lbq
                        elif lk < 0:
                            use_carry, cblk = True, 4 + lk
                        if kind == "A":
                            if use_carry:
                                return (kTc[r0:r0 + 64, pair_idx, cblk * 128:(cblk + 1) * 128],
                                        vca[:, cblk, pair_idx * 2 + hh, 0:65], [("kTc", pair_idx)], [("vca_c", pair_idx)])
                            return (kT_w[wb][r0:r0 + 64, lk * 128:(lk + 1) * 128], v_w[wb][:, lk, hh, 0:65],
                                    [("kTw", wb)], [("vw", wb, lk)])
                        g = pair_idx // 2
                        gch, gh = g // 2, g % 2
                        ch = gch if hh == gh else 2 + gch
                        if use_carry:
                            return (kTc[r0:r0 + 64, 8 + ch, cblk * 128:(cblk + 1) * 128], vcb[:, g, 0:65],
                                    [("kTc", 8)], ["vcb_c"])
                        return (kTb_w[r0:r0 + 64, ch, lk * 128:(lk + 1) * 128], vb_w[:, lk, g, 0:65],
                                ["kTbw"], [("vbw", lk)])

                    state = {}

                    def front(lbq=lbq, hh=hh, kv_src=kv_src, state=state):
                        u = st_ctr[0]
                        st_ctr[0] += 1
                        sbuf = u % 2
                        pbuf = u % 3
                        s5 = u % 4
                        state["sbuf"] = sbuf
                        state["pbuf"] = pbuf
                        pc0 = sbuf * 512
                        for jj in range(nkb):
                            kap, vap, kres, vres = kv_src(jj)
                            if kind == "A" and jj == 4:
                                oc0, ores = 1024 + s5 * 128, ("pst", 2)
                            else:
                                oc0, ores = pc0 + jj * 128, ("pst", sbuf)
                            S.add("pe", lambda e, kap=kap, oc0=oc0: e.matmul(
                                out=pst[:, 0, oc0:oc0 + 128], lhsT=kap,
                                rhs=qT_w[wb][hh * 64:(hh + 1) * 64, lbq * 128:(lbq + 1) * 128],
                                start=True, stop=True), reads=kres + [("qTw", wb)], writes=[ores])
                        if kind == "A":
                            bap = bias_tile[:, hh, :]
                            S.add("dve", lambda e: e.tensor_tensor(out=t_w[sbuf][:, 0:512], in0=pst[:, 0, pc0:pc0 + 512],
                                                                   in1=bap[:, 0:512], op=ALU.add),
                                  reads=[("pst", sbuf), ("bias", kind, id(bias_tile))], writes=[("tw", sbuf)])
                            S.add("dve", lambda e: e.tensor_tensor(out=t_w[sbuf][:, 512:640],
                                                                   in0=pst[:, 0, 1024 + s5 * 128:1024 + (s5 + 1) * 128],
                                                                   in1=bap[:, 512:640], op=ALU.add),
                                  reads=[("pst", 2), ("bias", kind, id(bias_tile))], writes=[("tw5", sbuf)])
                        else:
                            bap = bias_tile[:, hh, :, :].rearrange("p a b -> p (a b)")
                            S.add("dve", lambda e: e.tensor_tensor(out=t_w[sbuf][:, 0:ncol], in0=pst[:, 0, pc0:pc0 + ncol],
                                                                   in1=bap, op=ALU.add),
                                  reads=[("pst", sbuf), ("bias", kind, id(bias_tile))], writes=[("tw", sbuf), ("tw5", sbuf)])
                        S.add("act", lambda e: e.activation(out=p_w[pbuf][:, 0:ncol], in_=t_w[sbuf][:, 0:ncol],
                                                            func=AF.Exp),
                              reads=[("tw", sbuf), ("tw5", sbuf)], writes=[("pw", pbuf)])

                    def back(lbq=lbq, hh=hh, kv_src=kv_src, state=state):
                        sbuf = state["sbuf"]
                        pbuf = state["pbuf"]
                        slot = pm_ctr[0] % 2
                        pm_ctr[0] += 1
                        rv = slot
                        for jj in range(nkb):
                            kap, vap, kres, vres = kv_src(jj)
                            S.add("pe", lambda e, jj=jj, vap=vap: e.matmul(
                                out=ppv[:, slot, 0:65], lhsT=p_w[pbuf][:, jj * 128:(jj + 1) * 128],
                                rhs=vap, start=(jj == 0), stop=(jj == nkb - 1)),
                                reads=vres + [("pw", pbuf)], writes=[("ppv", slot)])
                        ob = (lbq) % 2
                        if kind == "A":
                            S.add("dve", lambda e: e.reciprocal(out=rinv[rv][:, 0:1], in_=ppv[:, slot, 64:65]),
                                  reads=[("ppv", slot)], writes=[("rinv", rv)])
                        else:
                            hglob = pair_idx * 2 + hh
                            S.add("dve", lambda e: e.tensor_tensor(
                                out=rinv[rv][:, 0:1], in0=ppv[:, slot, 64:65],
                                in1=esink[:, hglob:hglob + 1], op=ALU.add),
                                reads=[("ppv", slot), "esink"], writes=[("rinv", rv)])
                            S.add("dve", lambda e: e.reciprocal(out=rinv[rv][:, 0:1], in_=rinv[rv][:, 0:1]),
                                  reads=[("rinv", rv)], writes=[("rinv", rv)])
                        S.add("act", lambda e: e.activation(
                            out=o_tm[ob][:, hh * 64:(hh + 1) * 64], in_=ppv[:, slot, 0:64], func=AF.Copy,
                            scale=rinv[rv][:, 0:1]),
                            reads=[("ppv", slot), ("rinv", rv)], writes=[("otm", ob, hh)])
                        state["slot"] = slot

                    def tail(lbq=lbq, hh=hh, state=state):
                        if hh != 1:
                            return
                        slot = state["slot"]
                        ob = (lbq) % 2
                        S.add("pe", lambda e: e.transpose(out=ppv[:, slot, 256:384], in_=o_tm[ob][:],
                                                          identity=ident[:]),
                              reads=[("otm", ob, 0), ("otm", ob, 1), "ident"], writes=[("ppv", slot)])
                        S.add("dve", lambda e: e.tensor_tensor(
                            out=oT[:, oc, lbq * 128:(lbq + 1) * 128], in0=ppv[:, slot, 256:384],
                            in1=gT_w[wb][:, lbq * 128:(lbq + 1) * 128], op=ALU.mult),
                            reads=[("ppv", slot), ("gTw", wb)], writes=[("oT", lbq)])

                    units.append((front, back, tail))
            return units

        def run_interleaved(units, fillers, LA=2, TD=1):
            nf = len(fillers)
            nu = len(units)
            fi = 0
            nslot = nu + LA + TD
            for u in range(nslot):
                if u < nu:
                    units[u][0]()
                target = (nf * (u + 1)) // nslot if nu > 0 else nf
                while fi < target:
                    fillers[fi]()
                    fi += 1
                if LA <= u < nu + LA:
                    units[u - LA][1]()
                if u >= LA + TD:
                    units[u - LA - TD][2]()
            while fi < nf:
                fillers[fi]()
                fi += 1

        for st, blocks in enumerate(ST_BLOCKS):
            nb = len(blocks)
            ncols = nb * 128
            if st == 0:
                qblocks = [4, 5, 6, 7]
                sample_lb = 7
                carry_from = [3, 4, 5, 6]
                kvout = {7: ("s", 0)}
            else:
                qblocks = list(range(nb))
                sample_lb = -1
                carry_from = [4, 5, 6, 7] if st < n_st - 1 else None
                kvout = {}
                if st == n_st - 1:
                    for lb, gb in enumerate(blocks):
                        if gb >= 32:
                            kvout[lb] = ("p", gb - 32)
            q0, q1 = qblocks[0] * 128, (qblocks[-1] + 1) * 128

            steps = [("A", p) for p in range(8)] + [("Bkv", 0)] + [("Bq", i) for i in range(8)]

            def weight_parts(kind, idx):
                if kind == "A":
                    return [(0, 128, w_in_v, OFF_QA + idx * 128), (128, 128, w_in_v, OFF_KA + idx * 128),
                            (256, 128, w_in_v, OFF_VA + idx * 128), (384, 128, w_in_v, OFF_GA + idx * 128)]
                if kind == "Bkv":
                    return [(0, 256, w_in_v, OFF_KB), (256, 256, w_in_v, OFF_VB)]
                if kind == "Bq":
                    return [(0, 128, w_in_v, OFF_QB + idx * 128), (128, 128, w_in_v, OFF_GB + idx * 128)]
                return [(0, 512, w_out_v, idx * 512)]

            def tm_proj(slot, wc0, wn, lb, evac):
                def item():
                    pres = ["pv"]
                    o0 = 0
                    for kc in range(16):
                        S.add("pe", lambda e, kc=kc: e.matmul(
                            out=pv[:, o0:o0 + wn], lhsT=hT[:, kc, lb * 128:(lb + 1) * 128],
                            rhs=wsl[slot][:, kc, wc0:wc0 + wn], start=(kc == 0), stop=(kc == 15)),
                            reads=[("wsl", slot), ("hT", lb)], writes=pres)
                    evac(o0, pres)
                return item

            def kv_store(kind, lb, st_buf, which, col0, ncol_out):
                tag, off = kvout[lb]
                if tag == "p":
                    if kind == "A":
                        dst = kvp_a[which, off * 128:(off + 1) * 128, col0:col0 + ncol_out]
                    else:
                        dst = kvp_b[which, 0:128, col0:col0 + ncol_out]
                    srcp = slice(0, 128)
                else:
                    base = 448 if kind == "A" else 64
                    dst = (kvs_a if kind == "A" else kvs_b)[which, base:base + 64, col0:col0 + ncol_out]
                    srcp = slice(0, 64)
                return dst, srcp

            def inproj_items(step_i, kind, idx, slot, wb):
                items = []
                if kind == "A":
                    def evac_k(b, n0, nn):
                        S.add("act", lambda e: e.activation(out=kT_w[wb][:, n0:n0 + nn], in_=pz[:, b, 0:nn], func=AF.Copy),
                              reads=[("pz", b)], writes=[("kTw", wb)])
                    def evac_q(b, n0, nn):
                        S.add("dve", lambda e: e.tensor_scalar(out=qT_w[wb][:, n0:n0 + nn], in0=pz[:, b, 0:nn],
                                                               scalar1=0.125, scalar2=None, op0=ALU.mult),
                              reads=[("pz", b)], writes=[("qTw", wb)])
                    def evac_g(b, n0, nn):
                        xb = xr_ctr[0] % 3
                        xr_ctr[0] += 1
                        S.add("act", lambda e: e.activation(out=xr[xb][:, 0:nn], in_=pz[:, b, 0:nn], func=AF.Tanh, scale=0.5),
                              reads=[("pz", b)], writes=[("xr", xb)])
                        S.add("dve", lambda e: e.scalar_tensor_tensor(out=gT_w[wb][:, n0:n0 + nn], in0=xr[xb][:, 0:nn], scalar=1.0,
                                                                      in1=pz[:, b, 0:nn], op0=ALU.add, op1=ALU.mult),
                              reads=[("pz", b), ("xr", xb)], writes=[("gTw", wb)])
                    items += fm_proj(slot, 128, 0, ncols, evac_k, hreads_blocks)
                    items += fm_proj(slot, 0, q0, q1, evac_q, hreads_blocks)
                    for lb in range(nb):
                        if lb in kvout:
                            def evac(o0, pres, lb=lb):
                                sbf = kvst_ctr[0] % 2
                                kvst_ctr[0] += 1
                                S.add("act", lambda e: e.activation(
                                    out=v_w[wb][:, lb, :, 0:64], in_=pv[:, o0 + 128:o0 + 256].rearrange("p (h d) -> p h d", d=64),
                                    func=AF.Copy),
                                    reads=pres + [("vw1", wb)], writes=[("vw", wb, lb)])
                                S.add("act", lambda e: e.activation(out=kvst[sbf][:, 0:256], in_=pv[:, o0:o0 + 256], func=AF.Copy),
                                      reads=pres, writes=[("kvst", sbf)])
                                for which in range(2):
                                    dst, srcp = kv_store("A", lb, sbf, which, idx * 128, 128)
                                    S.add("sp", lambda e, dst=dst, srcp=srcp, which=which: e.dma_start(
                                        out=dst, in_=kvst[sbf][srcp, which * 128:(which + 1) * 128]),
                                        reads=[("kvst", sbf)], dma=("kvo", sbf))
                            items.append(tm_proj(slot, 128, 256, lb, evac))
                    def evac_vT(b, n0, nn):
                        S.add("act", lambda e: e.activation(out=vT_w[:, n0:n0 + nn], in_=pz[:, b, 0:nn], func=AF.Copy),
                              reads=[("pz", b)], writes=[("vTw", n0)])
                    items += fm_proj(slot, 256, 0, ncols, evac_vT, hreads_blocks)
                    pv_bf = pv[:].bitcast(BF16)
                    nonkv = [lb for lb in range(nb) if lb not in kvout]
                    for i0 in range(0, len(nonkv), 2):
                        batch = nonkv[i0:i0 + 2]

                        def vt_item(batch=batch):
                            for bi, lb in enumerate(batch):
                                S.add("pe", lambda e, bi=bi, lb=lb: e.transpose(
                                    out=pv_bf[:, bi * 128:(bi + 1) * 128], in_=vT_w[:, lb * 128:(lb + 1) * 128],
                                    identity=ident_bf[:]),
                                    reads=[("vTw", (lb * 128) // 512 * 512), "ident_bf"], writes=["pv"])
                            consecutive = all(batch[i] + 1 == batch[i + 1] for i in range(len(batch) - 1))
                            if consecutive:
                                nbb = len(batch)
                                S.add("act", lambda e: e.activation(
                                    out=v_w[wb][:, batch[0]:batch[0] + nbb, :, 0:64],
                                    in_=pv_bf[:, 0:nbb * 128].rearrange("p (b h d) -> p b h d", b=nbb, d=64),
                                    func=AF.Copy),
                                    reads=["pv", ("vw1", wb)], writes=[("vw", wb, lb) for lb in batch])
                            else:
                                for bi, lb in enumerate(batch):
                                    S.add("act", lambda e, bi=bi, lb=lb: e.activation(
                                        out=v_w[wb][:, lb, :, 0:64],
                                        in_=pv_bf[:, bi * 128:(bi + 1) * 128].rearrange("p (h d) -> p h d", d=64),
                                        func=AF.Copy),
                                        reads=["pv", ("vw1", wb)], writes=[("vw", wb, lb)])
                        items.append(vt_item)
                    items += fm_proj(slot, 384, q0, q1, evac_g, hreads_blocks)
                elif kind == "Bkv":
                    for ch in range(2):
                        def evac_k(b, n0, nn, ch=ch):
                            S.add("act", lambda e: e.activation(out=kTb_w[:, ch, n0:n0 + nn], in_=pz[:, b, 0:nn], func=AF.Copy),
                                  reads=[("pz", b)], writes=[("kTbw0", ch, n0), "kTbw"])
                        items += fm_proj(slot, ch * 128, 0, ncols, evac_k, hreads_blocks)
                    for ch in range(2):
                        for (n0, nn) in groups_of(0, ncols):
                            def item(ch=ch, n0=n0, nn=nn):
                                b = pz_ctr[0] % 2
                                pz_ctr[0] += 1
                                S.add("pe", lambda e: e.matmul(out=pz[:, b, 0:nn], lhsT=perm[:],
                                                               rhs=kTb_w[:, ch, n0:n0 + nn], start=True, stop=True),
                                      reads=[("kTbw0", ch, n0), "perm"], writes=[("pz", b)])
                                S.add("dve", lambda e: e.tensor_copy(out=kTb_w[:, 2 + ch, n0:n0 + nn], in_=pz[:, b, 0:nn]),
                                      reads=[("pz", b)], writes=[("kTbw1", ch, n0), "kTbw"])
                            items.append(item)
                    for lb in range(nb):
                        if lb in kvout and (kvout[lb][0] == "s" or kvout[lb][1] == 3):
                            def evac(o0, pres, lb=lb):
                                sbf = kvst_ctr[0] % 2
                                kvst_ctr[0] += 1
                                S.add("act", lambda e: e.activation(
                                    out=vb_w[:, lb, :, 0:64], in_=pv[:, 256:512].rearrange("p (h d) -> p h d", d=64),
                                    func=AF.Copy),
                                    reads=pres + ["vbw1"], writes=[("vbw", lb)])
                                S.add("act", lambda e: e.activation(out=kvst[sbf][:, 0:512], in_=pv[:, 0:512], func=AF.Copy),
                                      reads=pres, writes=[("kvst", sbf)])
                                for which in range(2):
                                    dst, srcp = kv_store("B", lb, sbf, which, 0, 256)
                                    S.add("sp", lambda e, dst=dst, srcp=srcp, which=which: e.dma_start(
                                        out=dst, in_=kvst[sbf][srcp, which * 256:(which + 1) * 256]),
                                        reads=[("kvst", sbf)], dma=("kvo", sbf))
                            items.append(tm_proj(slot, 0, 512, lb, evac))
                        else:
                            def evac(o0, pres, lb=lb):
                                S.add("act", lambda e: e.activation(
                                    out=vb_w[:, lb, :, 0:64], in_=pv[:, o0:o0 + 256].rearrange("p (h d) -> p h d", d=64),
                                    func=AF.Copy),
                                    reads=pres + ["vbw1"], writes=[("vbw", lb)])
                            items.append(tm_proj(slot, 256, 256, lb, evac))
                else:
                    def evac_q(b, n0, nn):
                        S.add("dve", lambda e: e.tensor_scalar(out=qT_w[wb][:, n0:n0 + nn], in0=pz[:, b, 0:nn],
                                                               scalar1=0.125, scalar2=None, op0=ALU.mult),
                              reads=[("pz", b)], writes=[("qTw", wb)])
                    def evac_g(b, n0, nn):
                        xb = xr_ctr[0] % 3
                        xr_ctr[0] += 1
                        S.add("act", lambda e: e.activation(out=xr[xb][:, 0:nn], in_=pz[:, b, 0:nn], func=AF.Tanh, scale=0.5),
                              reads=[("pz", b)], writes=[("xr", xb)])
                        S.add("dve", lambda e: e.scalar_tensor_tensor(out=gT_w[wb][:, n0:n0 + nn], in0=xr[xb][:, 0:nn], scalar=1.0,
                                                                      in1=pz[:, b, 0:nn], op0=ALU.add, op1=ALU.mult),
                              reads=[("pz", b), ("xr", xb)], writes=[("gTw", wb)])
                    items += fm_proj(slot, 0, q0, q1, evac_q, hreads_blocks)
                    items += fm_proj(slot, 128, q0, q1, evac_g, hreads_blocks)
                return items

            def bias_load(kind, idx):
                if kind == "Bkv":
                    return None
                bt = bias_ctr[kind] % 2
                bias_ctr[kind] += 1
                if kind == "A":
                    tile = bA_w[bt]
                    S.add("sp", lambda e: e.dma_start(out=tile[:], in_=biasA_d[2 * idx:2 * idx + 2].rearrange("h p c -> p h c")),
                          reads=[("biasA_d", 2 * idx), ("biasA_d", 2 * idx + 1)],
                          writes=[("bias", "A", id(tile))], dma=("ba", bt))
                    return tile, idx
                tile = bB_w[bt]
                S.add("sp", lambda e: e.dma_start(out=tile[:], in_=alibi_d[:, 2 * idx:2 * idx + 2, :, :]),
                      writes=[("bias", "Bq", id(tile))], dma=("bb", bt))
                return tile, 8 + idx

            def attn_for(step_i, kind, idx, wb, bl):
                if kind == "Bkv":
                    return [], None
                tile, oc = bl
                units = attention_units(st, wb, "A" if kind == "A" else "Bq", idx, qblocks, sample_lb, tile, oc)
                return units, None

            def carry_update(kind, idx, wb):
                if carry_from is None:
                    return
                c0 = carry_from[0] * 128
                if kind == "A":
                    S.add("dve", lambda e: e.tensor_copy(out=kTc[:, idx, :], in_=kT_w[wb][:, c0:c0 + 512]),
                          reads=[("kTw", wb)], writes=[("kTc", idx)])
                    S.add("dve", lambda e: e.tensor_copy(out=vca[:, :, 2 * idx:2 * idx + 2, :],
                                                         in_=v_w[wb][:, carry_from[0]:carry_from[0] + 4, :, :]),
                          reads=[("vw", wb, lb) for lb in carry_from] + [("vw1", wb)], writes=[("vca_c", idx)])

            def carry_update_b():
                if carry_from is None:
                    return
                lbl = carry_from[-1]
                S.add("dve", lambda e: e.tensor_copy(out=kTc[:, 8:12, 384:512], in_=kTb_w[:, :, lbl * 128:(lbl + 1) * 128]),
                      reads=["kTbw"], writes=[("kTc", 8)])
                S.add("dve", lambda e: e.tensor_copy(out=vcb[:], in_=vb_w[:, lbl, :, :]),
                      reads=[("vbw", lbl), "vbw1"], writes=["vcb_c"])

            slots = {}
            slots[0] = load_weights(weight_parts(*steps[0]))
            prev_units = []
            prev_fin = None
            for si, (kind, idx) in enumerate(steps):
                gstep = step_ctr[0]
                step_ctr[0] += 1
                wb = gstep % 2
                slot = slots[si]
                if si + 1 < len(steps):
                    slots[si + 1] = load_weights(weight_parts(*steps[si + 1]))
                else:
                    slots[si + 1] = load_weights(weight_parts("C", 0))
                bl = bias_load(kind, idx)
                items = inproj_items(si, kind, idx, slot, wb)
                dwork = pending_D.pop(0) if pending_D else None
                if dwork:
                    dwork[0]()
                if interleave:
                    run_interleaved(prev_units, items)
                    if prev_fin:
                        prev_fin()
                else:
                    for it in items:
                        it()
                if dwork:
                    dwork[1]()
                if kind == "Bkv":
                    S.add("dve", lambda e: e.memset(stat[:, 63:64], 0.0),
                          reads=[("kTbw0", ch, n0) for ch in range(2) for (n0, nn) in groups_of(0, ncols)] +
                                [("kTbw1", ch, n0) for ch in range(2) for (n0, nn) in groups_of(0, ncols)],
                          writes=["kTbw"])
                units, _ = attn_for(si, kind, idx, wb, bl)
                fin = (lambda kind=kind, idx=idx, wb=wb: carry_update(kind, idx, wb)) if kind == "A" else None
                if interleave:
                    prev_units, prev_fin = units, fin
                else:
                    run_interleaved(units, [])
                    if fin:
                        fin()
                ckpt(10 + st * 100 + si)
            if interleave:
                run_interleaved(prev_units, [])
                if prev_fin:
                    prev_fin()
                prev_units, prev_fin = [], None
            carry_update_b()
            next_blocks = ST_BLOCKS[st + 1] if st + 1 < n_st else []
            a_parts = [phase_A_parts(gb, lb, alt=(lb % 2 == 1)) for lb, gb in enumerate(next_blocks)]
            a_pieces = []
            for k, (pdma, pcomp, groups) in enumerate(a_parts):
                for gi, g in enumerate(groups):
                    pre = None
                    if k + 1 < len(a_parts):
                        if gi == 0:
                            pre = a_parts[k + 1][0]
                        elif gi == 1:
                            pre = a_parts[k + 1][1]
                    a_pieces.append((pre, g))
            if a_parts:
                a_parts[0][0]()
                a_parts[0][1]()
            c_slots = {0: slots[len(steps)]}
            cl = []
            for j in range(4):
                for bi, lbq in enumerate(qblocks):
                    gb = blocks[lbq]
                    st8 = {}

                    def c_load(gb=gb, j=j, st8=st8):
                        xb = xr_ctr[0] % 3
                        xr_ctr[0] += 1
                        st8["xb"] = xb
                        S.add("sp", lambda e: e.dma_start(out=xr[xb][:], in_=xin[gb * 128:(gb + 1) * 128, j * 512:(j + 1) * 512]),
                              writes=[("xr", xb)], dma=("xr", xb))

                    def c_main(lbq=lbq, gb=gb, j=j, bi=bi, st8=st8):
                        if bi == 0 and j + 1 < 4:
                            c_slots[j + 1] = load_weights(weight_parts("C", j + 1))
                        slot = c_slots[j]
                        xb = st8["xb"]
                        b = pz_ctr[0] % 2
                        pz_ctr[0] += 1
                        npart = 64 if gb == SAMPLE_BLK else 128
                        yrow = 4096 if gb == SAMPLE_BLK else (gb - 4) * 128
                        for kc in range(16):
                            S.add("pe", lambda e, kc=kc: e.matmul(
                                out=pz[:, b, :], lhsT=oT[:, kc, lbq * 128:(lbq + 1) * 128], rhs=wsl[slot][:, kc, :],
                                start=(kc == 0), stop=(kc == 15)),
                                reads=[("wsl", slot), ("oT", lbq)], writes=[("pz", b)])
                        S.add("dve", lambda e: e.tensor_tensor(out=xr[xb][:], in0=pz[:, b, :], in1=xr[xb][:], op=ALU.add),
                              reads=[("pz", b), ("xr", xb)], writes=[("xr", xb)])
                        S.add("act", lambda e: e.activation(out=junk[:], in_=xr[xb][:], func=AF.Square,
                                                            accum_out=ssC[:, bi, j:j + 1]),
                              reads=[("xr", xb)], writes=["junk", ("ssC", bi, j)])
                        S.add("act", lambda e: e.dma_start(out=y_d[yrow:yrow + npart, j * 512:(j + 1) * 512],
                                                           in_=xr[xb][0:npart, :]),
                              reads=[("xr", xb)], writes=[("yd", gb, j)], dma=("rs", xb))
                    cl.append((c_load, c_main))
            nci = len(cl)
            na = len(a_pieces)
            for i in range(min(2, nci)):
                cl[i][0]()
            adone = 0
            for i in range(nci):
                if i + 2 < nci:
                    cl[i + 2][0]()
                cl[i][1]()
                tgt = (na * (i + 1)) // nci
                while adone < tgt:
                    pre, g = a_pieces[adone]
                    if pre:
                        pre()
                    g()
                    adone += 1
            while adone < na:
                pre, g = a_pieces[adone]
                if pre:
                    pre()
                g()
                adone += 1
            ckpt(10 + st * 100 + 50)
            nq = len(qblocks)
            S.add("dve", lambda e: e.tensor_reduce(out=stat[:, 48:48 + nq], in_=ssC[:, 0:nq, :], axis=AX.X, op=ALU.add),
                  reads=[("ssC", bi, j) for bi in range(nq) for j in range(4)], writes=["stD"])
            S.add("dve", lambda e: e.tensor_scalar(out=stat[:, 48:48 + nq], in0=stat[:, 48:48 + nq], scalar1=1.0 / D,
                                                   scalar2=EPS, op0=ALU.mult, op1=ALU.add),
                  reads=["stD"], writes=["stD"])
            S.add("act", lambda e: e.activation(out=stat[:, 48:48 + nq], in_=stat[:, 48:48 + nq], func=AF.Sqrt),
                  reads=["stD"], writes=["stD"])
            S.add("dve", lambda e: e.reciprocal(out=stat[:, 48:48 + nq], in_=stat[:, 48:48 + nq]),
                  reads=["stD"], writes=["stD"])
            for bi, lbq in enumerate(qblocks):
                gb = blocks[lbq]
                dst = {}

                def d_load(bi=bi, gb=gb, dst=dst):
                    npart = 64 if gb == SAMPLE_BLK else 128
                    yrow = 4096 if gb == SAMPLE_BLK else (gb - 4) * 128
                    dst["cr"] = 48 + bi
                    S.add("sp", lambda e: e.dma_start(out=rD[0:npart, :], in_=y_d[yrow:yrow + npart, :]),
                          reads=[("yd", gb, j) for j in range(4)], writes=["rD"], dma="dl")

                def d_rest(bi=bi, gb=gb, dst=dst):
                    npart = 64 if gb == SAMPLE_BLK else 128
                    yrow = 4096 if gb == SAMPLE_BLK else (gb - 4) * 128
                    cr = dst["cr"]
                    S.add("dve", lambda e: e.scalar_tensor_tensor(out=rD[:], in0=rD[:], scalar=stat[:, cr:cr + 1],
                                                                  in1=gfin[:], op0=ALU.mult, op1=ALU.mult),
                          reads=["rD", "stD", "gfin"], writes=["rD"])
                    S.add("sp", lambda e: e.dma_start(out=y_d[yrow:yrow + npart, :], in_=rD[0:npart, :]),
                          reads=["rD"], writes=[("yd", gb, 0)], dma="ds")
                pending_D.append((d_load, d_rest))
            if st == n_st - 1:
                while pending_D:
                    dl, dr = pending_D.pop(0)
                    dl()
                    dr()
            if st == 0:
                for i in range(2):
                    S.add("dve", lambda e, i=i: e.memset(v_w[i][:, 0:4, :, 64:66], 2.0),
                          reads=[("vw", i, lb) for lb in range(4)], writes=[("vw1", i)])
                S.add("dve", lambda e: e.memset(vb_w[:, 0:4, :, 64:66], 2.0),
                      reads=[("vbw", lb) for lb in range(4)], writes=["vbw1"])

        S.emit(nc)
    return nc


_CACHE = {}


def _host_consts():
    p = np.arange(128)[:, None, None]
    jj = np.arange(5)[None, :, None]
    f = np.arange(128)[None, None, :]
    maskA = np.zeros((128, 5, 128), np.float32)
    maskA[(jj == 0) & (p < 64) & (f >= 64) | (jj == 4) & (p >= 64) & (f < 64)] = NEG
    slopes = (2.0 ** (-8.0 * np.arange(1, 17, dtype=np.float32) / 16)).astype(np.float32)
    j2 = np.arange(2)[None, :, None]
    d = (128 + f - 128 * j2 - p).astype(np.float32)
    al = -slopes[None, :, None, None] * np.abs(d)[:, None, :, :]
    mB = ((j2 == 0) & (p < 64) & (f >= 64)) | ((j2 == 1) & (p >= 64) & (f < 64))
    al = np.where(mB[:, None, :, :], np.float32(NEG), al).astype(np.float32)
    ident = np.eye(128, dtype=np.float32)
    perm = np.roll(np.eye(128, dtype=np.float32), 64, axis=1)
    return maskA, np.ascontiguousarray(al), ident, np.ascontiguousarray(perm)


def kernel(x_prompt, x_sample, cache_a_k, cache_a_v, cache_b_k, cache_b_v,
           norm_in, w_in, rel_bias_a, sinks_b, w_out, norm_final):
    f32 = np.float32
    x_prompt = np.asarray(x_prompt, f32)
    x_sample = np.asarray(x_sample, f32)
    if "nc" not in _CACHE:
        _CACHE["nc"] = build_program()
    nc = _CACHE["nc"]
    maskA, alibi, ident, perm = _host_consts()
    w_in2 = np.ascontiguousarray(np.asarray(w_in, f32)[0])
    w_out2 = np.ascontiguousarray(np.asarray(w_out, f32)[0])
    g_inT = np.ascontiguousarray(np.asarray(norm_in, f32)[0].reshape(16, 128).T)
    gfin = np.ascontiguousarray(np.broadcast_to(np.asarray(norm_final, f32)[None, :], (128, D)))
    relb = np.ascontiguousarray(np.asarray(rel_bias_a, f32)[0])
    sinks = np.ascontiguousarray(np.broadcast_to(np.asarray(sinks_b, f32)[0][None, :], (128, 16)))
    in_maps = []
    for c in range(8):
        b, qd = c // 4, c % 4
        xin = np.zeros((XROWS, D), f32)
        if qd > 0:
            xin[0:512] = x_prompt[b, qd * 4096 - 512:qd * 4096]
        xin[512:512 + 4096] = x_prompt[b, qd * 4096:(qd + 1) * 4096]
        xin[SAMPLE_BLK * 128:SAMPLE_BLK * 128 + 64] = x_sample[c]
        hv = np.full((128, 4), 2.0 if qd > 0 else 0.0, f32)
        in_maps.append({
            "xin": xin, "w_in": w_in2, "w_out": w_out2, "g_inT": g_inT, "gfin": gfin, "relb": relb,
            "sinks": sinks,
            "cak": np.ascontiguousarray(np.asarray(cache_a_k, f32)[0, c].reshape(512, 1024)),
            "cav": np.ascontiguousarray(np.asarray(cache_a_v, f32)[0, c].reshape(512, 1024)),
            "cbk": np.ascontiguousarray(np.asarray(cache_b_k, f32)[0, c].reshape(128, 256)),
            "cbv": np.ascontiguousarray(np.asarray(cache_b_v, f32)[0, c].reshape(128, 256)),
            "hvalid": hv, "alibi": alibi, "maskA": maskA, "ident": ident, "perm": perm,
            "antiI": np.ascontiguousarray(ident[::-1]),
        })
    if _CACHE.get("only_maps"):
        return in_maps
    res = run_bass_kernel_spmd(nc, in_maps, core_ids=list(range(8)))
    R = res.results
    y_prompt = np.empty((2, 16384, D), f32)
    y_sample = np.empty((8, 64, D), f32)
    for c in range(8):
        b, qd = c // 4, c % 4
        y_prompt[b, qd * 4096:(qd + 1) * 4096] = R[c]["y"][0:4096]
        y_sample[c] = R[c]["y"][4096:4160]
    akp = np.stack([R[3]["kvp_a"][0], R[7]["kvp_a"][0]]).reshape(1, 2, 512, 16, 64)
    avp = np.stack([R[3]["kvp_a"][1], R[7]["kvp_a"][1]]).reshape(1, 2, 512, 16, 64)
    bkp = np.stack([R[3]["kvp_b"][0], R[7]["kvp_b"][0]]).reshape(1, 2, 128, 4, 64)
    bvp = np.stack([R[3]["kvp_b"][1], R[7]["kvp_b"][1]]).reshape(1, 2, 128, 4, 64)
    aks = np.stack([R[c]["kvs_a"][0] for c in range(8)]).reshape(1, 8, 512, 16, 64)
    avs = np.stack([R[c]["kvs_a"][1] for c in range(8)]).reshape(1, 8, 512, 16, 64)
    bks = np.stack([R[c]["kvs_b"][0] for c in range(8)]).reshape(1, 8, 128, 4, 64)
    bvs = np.stack([R[c]["kvs_b"][1] for c in range(8)]).reshape(1, 8, 128, 4, 64)
    return (y_prompt, y_sample, akp, avp, bkp, bvp, aks, avs, bks, bvs)
```
